# Optimizing a Trainium2 kernel written in Bass

```python
import numpy as np
import jax
import jax.numpy as jnp
from jax import lax

D_MODEL = 1024
BATCH = 8
SEQ = 2048
DEPTH = 4

HEAD_DIM = 64
NSA_HEADS = 8
NSA_KV_HEADS = 2
NSA_GROUP = NSA_HEADS // NSA_KV_HEADS
SB_HEADS = 8
CMP_BLOCK = 32
CMP_STRIDE = 16
CMP_HIDDEN = 128
SEL_BLOCK = 64
SEL_TOPK = 16
WINDOW = 512
Q_BLOCK = 128
ROPE_THETA = 500000.0
ROT_DIM = HEAD_DIM // 4
D_FF = 2816
CONV_W = 3
LN_EPS = 1e-5
NEG = -1e30
FORCE = 1e4
DEEPNORM_ALPHA = (2.0 * DEPTH) ** 0.25
DEEPNORM_BETA = (8.0 * DEPTH) ** -0.25

NSA_Q = NSA_HEADS * HEAD_DIM
NSA_KV = NSA_KV_HEADS * HEAD_DIM
SB_W = SB_HEADS * HEAD_DIM
SPLIT_SIZES = (NSA_Q, NSA_KV, NSA_KV, NSA_KV, NSA_KV, NSA_KV, NSA_KV, 3 * NSA_HEADS, SB_W, SB_W, SB_W, 2 * D_MODEL)
SPLIT_POINTS = tuple(int(v) for v in np.cumsum(SPLIT_SIZES)[:-1])
IN_WIDTH = int(sum(SPLIT_SIZES))

kernel_name = "nsa_stickbreaking_gated_merge_deepnorm"


def layer_norm(x, g, b):
    xf = x.astype(jnp.float32)
    mu = jnp.mean(xf, axis=-1, keepdims=True)
    var = jnp.mean(jnp.square(xf - mu), axis=-1, keepdims=True)
    y = (xf - mu) * lax.rsqrt(var + LN_EPS)
    return (y * g.astype(jnp.float32) + b.astype(jnp.float32)).astype(x.dtype)


def rotary_tables(seq):
    inv_freq = ROPE_THETA ** (-np.arange(0, ROT_DIM, 2, dtype=np.float32) / ROT_DIM)
    ang = jnp.arange(seq, dtype=jnp.float32)[:, None] * jnp.asarray(inv_freq, jnp.float32)[None, :]
    return jnp.cos(ang), jnp.sin(ang)


def partial_rope(x, cos, sin):
    half = ROT_DIM // 2
    c = cos[None, :, None, :].astype(x.dtype)
    s = sin[None, :, None, :].astype(x.dtype)
    x1 = x[..., :half]
    x2 = x[..., half:ROT_DIM]
    return jnp.concatenate([x1 * c - x2 * s, x2 * c + x1 * s, x[..., ROT_DIM:]], axis=-1)


def masked_softmax(s, mask):
    s = jnp.where(mask, s.astype(jnp.float32), NEG)
    m = jnp.max(s, axis=-1, keepdims=True)
    e = jnp.where(mask, jnp.exp(s - m), 0.0)
    return e / jnp.maximum(jnp.sum(e, axis=-1, keepdims=True), 1e-30)


def compress_blocks(k, pos, w1, b1, w2):
    b, g, s, d = k.shape
    n_cmp = (s - CMP_BLOCK) // CMP_STRIDE + 1
    idx = CMP_STRIDE * np.arange(n_cmp)[:, None] + np.arange(CMP_BLOCK)[None, :]
    blocks = k[:, :, idx] + pos.astype(k.dtype)
    hid = jax.nn.gelu(blocks.reshape(b, g, n_cmp, CMP_BLOCK * d) @ w1 + b1)
    return hid @ w2


def selection_overlap(seq):
    n_cmp = (seq - CMP_BLOCK) // CMP_STRIDE + 1
    n_sel = seq // SEL_BLOCK
    cs = np.arange(n_cmp) * CMP_STRIDE
    ce = cs + CMP_BLOCK
    ss = np.arange(n_sel) * SEL_BLOCK
    se = ss + SEL_BLOCK
    ov = np.clip(np.minimum(ce[:, None], se[None, :]) - np.maximum(cs[:, None], ss[None, :]), 0, None)
    return jnp.asarray(ov / CMP_BLOCK, dtype=jnp.float32)


def gather_blocks(blocks, idx):
    g = jax.vmap(jax.vmap(lambda bl, ix: bl[ix]))(blocks, idx)
    b, gg, tq, n, sb, d = g.shape
    return g.reshape(b, gg, tq, n * sb, d)


def nsa_attention(q, k_cmp, v_cmp, k_sel, v_sel, k_win, v_win, gates,
                  pos_k, w1_k, b1_k, w2_k, pos_v, w1_v, b1_v, w2_v):
    b, _, s, d = q.shape
    scale = d ** -0.5
    qg = q.reshape(b, NSA_KV_HEADS, NSA_GROUP, s, d)
    t = jnp.arange(s)

    kc = compress_blocks(k_cmp, pos_k, w1_k, b1_k, w2_k)
    vc = compress_blocks(v_cmp, pos_v, w1_v, b1_v, w2_v)
    n_cmp = kc.shape[2]
    cmp_end = CMP_STRIDE * jnp.arange(n_cmp) + CMP_BLOCK - 1
    m_cmp = cmp_end[None, :] <= t[:, None]
    p_cmp = masked_softmax(jnp.einsum('bghsd,bgcd->bghsc', qg, kc) * scale, m_cmp)
    o_cmp = jnp.einsum('bghsc,bgcd->bghsd', p_cmp.astype(vc.dtype), vc)

    n_sel = s // SEL_BLOCK
    n_top = min(SEL_TOPK, n_sel)
    score = jnp.einsum('bghsc,cj->bgsj', p_cmp, selection_overlap(s))
    j = jnp.arange(n_sel)[None, :]
    cur = t[:, None] // SEL_BLOCK
    forced = (j == 0) | (j == cur) | (j == cur - 1)
    valid = j * SEL_BLOCK <= t[:, None]
    score = jnp.where(forced, FORCE, jnp.where(valid, score, -FORCE))
    _, sel_idx = lax.top_k(score, n_top)

    ks_blocks = k_sel.reshape(b, NSA_KV_HEADS, n_sel, SEL_BLOCK, d)
    vs_blocks = v_sel.reshape(b, NSA_KV_HEADS, n_sel, SEL_BLOCK, d)
    kw_pad = jnp.pad(k_win, ((0, 0), (0, 0), (WINDOW, 0), (0, 0)))
    vw_pad = jnp.pad(v_win, ((0, 0), (0, 0), (WINDOW, 0), (0, 0)))

    nb = s // Q_BLOCK
    q_blk = qg.reshape(b, NSA_KV_HEADS, NSA_GROUP, nb, Q_BLOCK, d).transpose(3, 0, 1, 2, 4, 5)
    idx_blk = sel_idx.reshape(b, NSA_KV_HEADS, nb, Q_BLOCK, n_top).transpose(2, 0, 1, 3, 4)
    t0s = jnp.arange(nb) * Q_BLOCK

    def query_block(args):
        qb, ib, t0 = args
        tq = t0 + jnp.arange(Q_BLOCK)
        kg = gather_blocks(ks_blocks, ib)
        vg = gather_blocks(vs_blocks, ib)
        kpos = (ib[..., None] * SEL_BLOCK + jnp.arange(SEL_BLOCK)).reshape(b, NSA_KV_HEADS, Q_BLOCK, n_top * SEL_BLOCK)
        m_sel = (kpos <= tq[:, None])[:, :, None]
        p = masked_softmax(jnp.einsum('bghqd,bgqkd->bghqk', qb, kg) * scale, m_sel)
        o_sel = jnp.einsum('bghqk,bgqkd->bghqd', p.astype(vg.dtype), vg)
        kwb = lax.dynamic_slice_in_dim(kw_pad, t0, WINDOW + Q_BLOCK, axis=2)
        vwb = lax.dynamic_slice_in_dim(vw_pad, t0, WINDOW + Q_BLOCK, axis=2)
        kp = t0 - WINDOW + jnp.arange(WINDOW + Q_BLOCK)
        diff = tq[:, None] - kp[None, :]
        m_win = (kp[None, :] >= 0) & (diff >= 0) & (diff < WINDOW)
        p = masked_softmax(jnp.einsum('bghqd,bgkd->bghqk', qb, kwb) * scale, m_win)
        o_win = jnp.einsum('bghqk,bgkd->bghqd', p.astype(vwb.dtype), vwb)
        return o_sel, o_win

    o_sel, o_win = lax.map(query_block, (q_blk, idx_blk, t0s))

    def unblock(o):
        return o.transpose(1, 2, 3, 0, 4, 5).reshape(b, NSA_KV_HEADS, NSA_GROUP, s, d)

    g = gates.reshape(b, NSA_KV_HEADS, NSA_GROUP, s, 3)
    o = g[..., 0:1] * o_cmp + g[..., 1:2] * unblock(o_sel) + g[..., 2:3] * unblock(o_win)
    return o.reshape(b, NSA_HEADS, s, d)


def stick_breaking_attention(q, k, v):
    s, d = q.shape[2], q.shape[3]
    scale = d ** -0.5
    outs = []
    for blk in range(s // Q_BLOCK):
        t0, t1 = blk * Q_BLOCK, (blk + 1) * Q_BLOCK
        z = jnp.einsum('bhqd,bhkd->bhqk', q[:, :, t0:t1], k[:, :, :t1]).astype(jnp.float32) * scale
        tq = t0 + jnp.arange(Q_BLOCK)
        mask = jnp.arange(t1)[None, :] < tq[:, None]
        log_fail = jnp.where(mask, jax.nn.log_sigmoid(-z), 0.0)
        suffix = lax.cumsum(log_fail, axis=3, reverse=True) - log_fail
        a = jnp.where(mask, jnp.exp(jax.nn.log_sigmoid(z) + suffix), 0.0)
        outs.append(jnp.einsum('bhqk,bhkd->bhqd', a.astype(v.dtype), v[:, :, :t1]))
    return jnp.concatenate(outs, axis=2)


def token_mixer(h, cos, sin, w_in, pos_k, w1_k, b1_k, w2_k, pos_v, w1_v, b1_v, w2_v,
                w_branch_a, w_branch_b, w_out):
    b, s, _ = h.shape
    (q_a, kc, vc, ks, vs, kw, vw, g_a, q_b, k_b, v_b, g_m) = jnp.split(h @ w_in, SPLIT_POINTS, axis=-1)

    def heads(t, n, rotary):
        t = t.reshape(b, s, n, HEAD_DIM)
        if rotary:
            t = partial_rope(t, cos, sin)
        return t.transpose(0, 2, 1, 3)

    nsa_gates = jax.nn.sigmoid(g_a).reshape(b, s, NSA_HEADS, 3).transpose(0, 2, 1, 3)
    o_a = nsa_attention(heads(q_a, NSA_HEADS, True),
                        heads(kc, NSA_KV_HEADS, True), heads(vc, NSA_KV_HEADS, False),
                        heads(ks, NSA_KV_HEADS, True), heads(vs, NSA_KV_HEADS, False),
                        heads(kw, NSA_KV_HEADS, True), heads(vw, NSA_KV_HEADS, False),
                        nsa_gates, pos_k, w1_k, b1_k, w2_k, pos_v, w1_v, b1_v, w2_v)
    o_b = stick_breaking_attention(heads(q_b, SB_HEADS, False), heads(k_b, SB_HEADS, False),
                                   heads(v_b, SB_HEADS, False))
    o_a = o_a.transpose(0, 2, 1, 3).reshape(b, s, NSA_Q)
    o_b = o_b.transpose(0, 2, 1, 3).reshape(b, s, SB_W)
    gm = jax.nn.sigmoid(g_m)
    merged = gm[..., :D_MODEL] * (o_a @ w_branch_a) + gm[..., D_MODEL:] * (o_b @ w_branch_b)
    return merged @ w_out


def conv_ffn(h, w_up, conv_w, conv_b, w_down):
    u = h @ w_up
    c = u.shape[-1]
    u = lax.conv_general_dilated(u, conv_w[:, None, :].astype(u.dtype), window_strides=(1,),
                                 padding=[(CONV_W - 1, 0)], dimension_numbers=('NWC', 'WIO', 'NWC'),
                                 feature_group_count=c) + conv_b
    a, v = jnp.split(u, 2, axis=-1)
    return (jax.nn.silu(a) * v) @ w_down


def setup_inputs(seed: int = 0) -> dict:
    key = jax.random.key(seed)
    ks = jax.random.split(key, 24)
    L, D = DEPTH, D_MODEL

    def nrm(k, shape, scale):
        return jax.random.normal(k, shape, jnp.float32) * scale

    cmp_in = CMP_BLOCK * HEAD_DIM
    return {
        "x": nrm(ks[0], (BATCH, SEQ, D), 1.0),
        "w_in": nrm(ks[1], (L, D, IN_WIDTH), D ** -0.5),
        "cmp_pos_k": nrm(ks[2], (L, CMP_BLOCK, HEAD_DIM), 0.02),
        "cmp_w1_k": nrm(ks[3], (L, cmp_in, CMP_HIDDEN), cmp_in ** -0.5),
        "cmp_b1_k": nrm(ks[4], (L, CMP_HIDDEN), 0.02),
        "cmp_w2_k": nrm(ks[5], (L, CMP_HIDDEN, HEAD_DIM), CMP_HIDDEN ** -0.5),
        "cmp_pos_v": nrm(ks[6], (L, CMP_BLOCK, HEAD_DIM), 0.02),
        "cmp_w1_v": nrm(ks[7], (L, cmp_in, CMP_HIDDEN), cmp_in ** -0.5),
        "cmp_b1_v": nrm(ks[8], (L, CMP_HIDDEN), 0.02),
        "cmp_w2_v": nrm(ks[9], (L, CMP_HIDDEN, HEAD_DIM), CMP_HIDDEN ** -0.5),
        "w_branch_a": nrm(ks[10], (L, NSA_Q, D), NSA_Q ** -0.5),
        "w_branch_b": nrm(ks[11], (L, SB_W, D), SB_W ** -0.5),
        "w_out": nrm(ks[12], (L, D, D), D ** -0.5 * DEEPNORM_BETA),
        "ln_mix_g": 1.0 + nrm(ks[13], (L, D), 0.02),
        "ln_mix_b": nrm(ks[14], (L, D), 0.02),
        "w_up": nrm(ks[15], (L, D, 2 * D_FF), D ** -0.5),
        "conv_w": nrm(ks[16], (L, CONV_W, 2 * D_FF), CONV_W ** -0.5),
        "conv_b": nrm(ks[17], (L, 2 * D_FF), 0.02),
        "w_down": nrm(ks[18], (L, D_FF, D), D_FF ** -0.5 * DEEPNORM_BETA),
        "ln_ffn_g": 1.0 + nrm(ks[19], (L, D), 0.02),
        "ln_ffn_b": nrm(ks[20], (L, D), 0.02),
    }


def reference(x, w_in, cmp_pos_k, cmp_w1_k, cmp_b1_k, cmp_w2_k, cmp_pos_v, cmp_w1_v, cmp_b1_v, cmp_w2_v,
              w_branch_a, w_branch_b, w_out, ln_mix_g, ln_mix_b, w_up, conv_w, conv_b, w_down,
              ln_ffn_g, ln_ffn_b):
    cos, sin = rotary_tables(x.shape[1])
    for l in range(DEPTH):
        y = token_mixer(x, cos, sin, w_in[l], cmp_pos_k[l], cmp_w1_k[l], cmp_b1_k[l], cmp_w2_k[l],
                        cmp_pos_v[l], cmp_w1_v[l], cmp_b1_v[l], cmp_w2_v[l],
                        w_branch_a[l], w_branch_b[l], w_out[l])
        x = layer_norm(DEEPNORM_ALPHA * x + y, ln_mix_g[l], ln_mix_b[l])
        y = conv_ffn(x, w_up[l], conv_w[l], conv_b[l], w_down[l])
        x = layer_norm(DEEPNORM_ALPHA * x + y, ln_ffn_g[l], ln_ffn_b[l])
    return x
```

```python
import numpy as np
from contextlib import ExitStack
import ml_dtypes
import concourse.bass as bass
import concourse.mybir as mybir
from concourse.bass_utils import run_bass_kernel_spmd

F32 = mybir.dt.float32
BF16 = mybir.dt.bfloat16
AF = mybir.ActivationFunctionType
ALU = mybir.AluOpType

S_LEN = 2048
D = 1024
NT = 16
NTC = 4
DEPTH = 4
D_FF = 2816
NJ = 22
ALPHA = float((2.0 * DEPTH) ** 0.25)
EPS = 1e-5
BIG = 1.0e4
IN_W = 3736
OFF_FM_NSA = 0
OFF_TM_NSA = 1920
OFF_FM_SB = 2200
OFF_TM_SB = 3224
OFF_GM = 3736


_CUR_JOIN = [None]


class Dep:
    __slots__ = ("writer", "readers")

    def __init__(self):
        self.writer = _CUR_JOIN[0]
        self.readers = []


def _nfree(ap):
    n = 1
    for d in ap.shape[1:]:
        n *= d
    return n


class Sched:
    ENG = ("pe", "act", "dve", "pool", "sp")
    N_DMA_SEM = 24
    XLAT = 150.0

    def __init__(self, nc, stack, reorder=True):
        self.nc = nc
        self.reorder = reorder
        self.ops = {e: [] for e in self.ENG}
        self.count = {e: 0 for e in self.ENG}
        self.known = {e: {} for e in self.ENG}
        self.semh = {}
        for e in self.ENG:
            self.semh[e] = stack.enter_context(nc.semaphore("tl_" + e))
        self.dma_i = {"sp": 0, "pool": 0}
        self.dma_last = {}
        for q in ("sp", "pool"):
            for k in range(self.N_DMA_SEM):
                self.semh[("dma", q, k)] = stack.enter_context(nc.semaphore("dma_%s_%d" % (q, k)))
        self.final_toks = []
        self.pe_rate = 0.42
        self.nodes = []
        self.seg_base = 0
        self.total = 0

    def _preds(self, reads, writes):
        base = self.seg_base
        p = set()
        for d in reads:
            if d.writer is not None and d.writer >= base:
                p.add(d.writer - base)
        for d in writes:
            if d.writer is not None and d.writer >= base:
                p.add(d.writer - base)
            for r in d.readers:
                if r >= base:
                    p.add(r - base)
        return p

    def _mark(self, gid, reads, writes):
        for d in reads:
            d.readers.append(gid)
        for d in writes:
            d.writer = gid
            d.readers = []

    def _cost(self, eng, meth, kw):
        if eng == "pe":
            if meth == "matmul":
                return 25.0 + self.pe_rate * max(_nfree(kw["rhs"]), 64)
            return 110.0
        ap = kw.get("out", kw.get("ap"))
        n = _nfree(ap) if ap is not None else 64
        if eng == "act":
            if meth == "copy":
                return 150.0 + 0.8 * n
            return 200.0 + 0.85 * n
        if eng == "dve":
            return 150.0 + 1.07 * n
        return 300.0 + 2.0 * n

    def op(self, eng, meth, kw, reads=(), writes=()):
        p = self._preds(reads, writes)
        gid = self.seg_base + len(self.nodes)
        self.nodes.append([eng, meth, kw, p, self._cost(eng, meth, kw), False, False])
        self._mark(gid, reads, writes)
        return gid

    def dma(self, q, out, in_, reads=(), writes=(), final=False):
        p = self._preds(reads, writes)
        gid = self.seg_base + len(self.nodes)
        nbytes = 1
        for d in out.shape:
            nbytes *= d
        nbytes *= 4
        lat = 2500.0 + nbytes / 150.0
        self.nodes.append([q, "dma_start", dict(out=out, in_=in_), p, lat, True, final])
        self._mark(gid, reads, writes)
        return gid

    def _order(self):
        nodes = self.nodes
        n = len(nodes)
        if not self.reorder:
            return list(range(n))
        succ = [[] for _ in range(n)]
        npend = [0] * n
        for i, nd in enumerate(nodes):
            npend[i] = len(nd[3])
            for p in nd[3]:
                succ[p].append(i)
        cpl = [0.0] * n
        for i in range(n - 1, -1, -1):
            m = 0.0
            for s_ in succ[i]:
                if cpl[s_] > m:
                    m = cpl[s_]
            cpl[i] = m + nodes[i][4]
        rtime = [0.0] * n
        finish = [0.0] * n
        ready = {e: [] for e in self.ENG}
        for i, nd in enumerate(nodes):
            if npend[i] == 0:
                ready[nd[0]].append(i)
        free_at = {e: 0.0 for e in self.ENG}
        order = []
        done = 0
        while done < n:
            best = None
            for e in self.ENG:
                rl = ready[e]
                if not rl:
                    continue
                fa = free_at[e]
                cand = None
                cand_t = None
                for i in rl:
                    t = rtime[i]
                    if t <= fa:
                        if cand is None or cand_t > fa or (cpl[i] > cpl[cand] + CP_SLACK) or (cpl[i] >= cpl[cand] - CP_SLACK and i < cand):
                            cand, cand_t = i, t
                    elif cand is None or (cand_t > fa and (t < cand_t or (t == cand_t and i < cand))):
                        cand, cand_t = i, t
                st = fa if cand_t <= fa else cand_t
                if best is None or st < best[0] or (st == best[0] and cand < best[2]):
                    best = (st, e, cand)
            assert best is not None, "scheduler deadlock"
            st, e, i = best
            ready[e].remove(i)
            nd = nodes[i]
            if nd[5]:
                free_at[e] = st + (60.0 if e == "sp" else 900.0)
                fin = st + nd[4]
            else:
                fin = st + nd[4]
                free_at[e] = fin
            finish[i] = fin
            order.append(i)
            done += 1
            for s_ in succ[i]:
                if nodes[s_][0] == e and not nd[5]:
                    lat = 0.0 if e == "pe" else 120.0
                else:
                    lat = self.XLAT
                if fin + lat > rtime[s_]:
                    rtime[s_] = fin + lat
                npend[s_] -= 1
                if npend[s_] == 0:
                    ready[nodes[s_][0]].append(s_)
        return order

    def flush(self):
        nodes = self.nodes
        if not nodes:
            return
        order = self._order()
        tok = [None] * len(nodes)
        for i in order:
            eng, meth, kw, preds, cost, is_dma, final = nodes[i]
            toks = {}
            for p in preds:
                t = tok[p]
                assert t is not None, "scheduler order violates a dependency"
                if toks.get(t[0], 0) < t[1]:
                    toks[t[0]] = t[1]
            waits = []
            kn = self.known[eng]
            for sk, v in toks.items():
                if sk == eng and eng == "pe":
                    continue
                if kn.get(sk, 0) >= v:
                    continue
                kn[sk] = v
                waits.append((sk, v))
            if is_dma:
                q = eng
                di = self.dma_i[q]
                self.dma_i[q] = di + 1
                sk = ("dma", q, di % self.N_DMA_SEM)
                val = 16 * (di // self.N_DMA_SEM + 1)
                if val > 16 and kn.get(sk, 0) < val - 16:
                    kn[sk] = val - 16
                    waits.append((sk, val - 16))
                tok[i] = (sk, val)
                self.dma_last[sk] = val
                self.ops[q].append((waits, (meth, kw), (sk, 16), None))
                if final:
                    self.final_toks.append(tok[i])
            else:
                c = self.count[eng] + 1
                self.count[eng] = c
                tok[i] = (eng, c)
                self.ops[eng].append((waits, (meth, kw), (eng, 1), c))
        self.seg_base += len(nodes)
        self.nodes = []

    def soft_barrier(self):
        base = self.seg_base
        start = getattr(self, "join_start", 0)
        preds = set(range(max(start - base, 0), len(self.nodes)))
        gid = base + len(self.nodes)
        self.nodes.append(["sp", "nop", {}, preds, 50.0, False, False])
        self.join_start = gid
        _CUR_JOIN[0] = gid

    def barrier(self):
        self.flush()
        toks = [(e, self.count[e]) for e in self.ENG if self.count[e] > 0]
        toks += list(self.dma_last.items())
        for e in self.ENG:
            kn = self.known[e]
            waits = []
            for sk, v in toks:
                if sk == e:
                    continue
                if kn.get(sk, 0) >= v:
                    continue
                kn[sk] = v
                waits.append((sk, v))
            if waits:
                self.ops[e].append((waits, None, None, None))

    def emit(self):
        self.flush()
        nc = self.nc
        fw = list(self.final_toks)
        waited = {e: set() for e in self.ENG}
        for e in self.ENG:
            for waits, fn, inc, pos in self.ops[e]:
                for sk, v in waits:
                    if sk in waited:
                        waited[sk].add(v)
        rank = {e: {p: r + 1 for r, p in enumerate(sorted(waited[e]))} for e in self.ENG}
        with nc.Block() as block:
            def mk(engname, extra=None):
                def f(e):
                    for waits, fn, inc, pos in self.ops[engname]:
                        for sk, v in waits:
                            if sk in rank:
                                v = rank[sk][v]
                            e.wait_ge(self.semh[sk], v)
                        if fn is None:
                            continue
                        ins = getattr(e, fn[0])(**fn[1])
                        if inc is None:
                            continue
                        if pos is None:
                            ins.then_inc(self.semh[inc[0]], inc[1])
                        elif pos in rank[engname]:
                            ins.then_inc(self.semh[engname], 1)
                    if extra:
                        for sk, v in extra:
                            e.wait_ge(self.semh[sk], v)
                return f
            block.tensor(mk("pe"))
            block.scalar(mk("act"))
            block.vector(mk("dve"))
            block.gpsimd(mk("pool"))
            block.sync(mk("sp", fw))
        self.n_milestones = {e: len(waited[e]) for e in self.ENG}


def _rot_cols(base):
    return list(range(base + 8, base + 16)) + list(range(base, base + 8)) + list(range(base + 16, base + 64))


def _win_col_index():
    idx = []
    for c in range(4):
        a = list(range(c * 64, c * 64 + 64)) + list(range((4 + c) * 64, (4 + c) * 64 + 64))
        r = _rot_cols(c * 64) + _rot_cols((4 + c) * 64)
        idx += a + r
    for base in (512, 768, 1024):
        a = list(range(base, base + 128))
        r = _rot_cols(base) + _rot_cols(base + 64)
        idx += a + r
    idx += list(range(640, 768))
    assert len(idx) == 1920
    idx += list(range(896, 1024)) + list(range(1152, 1280)) + list(range(1280, 1304))
    idx += list(range(1304, 1816)) + list(range(1816, 2328))
    idx += list(range(2328, 2840))
    assert len(idx) == IN_W
    return np.asarray(idx)


def _ffn_col_index():
    idx = []
    for j in range(NJ):
        idx += list(range(j * 128, j * 128 + 128)) + list(range(D_FF + j * 128, D_FF + j * 128 + 128))
    return np.asarray(idx)


def _constants():
    bf = ml_dtypes.bfloat16
    c = {}
    c["ident_b"] = np.eye(128, dtype=np.float32).astype(bf)
    c["ident_f"] = np.eye(128, dtype=np.float32)
    pr = np.zeros((128, 128), np.float32)
    prow = np.arange(128)
    for m_ in range(128):
        d = m_ % 64
        k_ = m_ + 8 if d < 8 else (m_ - 8 if d < 16 else m_)
        pr[k_, m_] = 1.0
        prow[m_] = k_
    c["perm_rot"] = pr.astype(bf)
    inv_freq = (np.float32(500000.0) ** (-np.arange(0, 16, 2, dtype=np.float32) / np.float32(16))).astype(np.float32)
    ang = (np.arange(S_LEN, dtype=np.float32)[:, None] * inv_freq[None, :]).astype(np.float32)
    cos = np.cos(ang).astype(np.float32)
    sin = np.sin(ang).astype(np.float32)
    rc = np.ones((128, S_LEN), np.float32)
    rs = np.zeros((128, S_LEN), np.float32)
    for r in range(128):
        d = r % 64
        if d < 8:
            rc[r] = cos[:, d]
            rs[r] = -sin[:, d]
        elif d < 16:
            rc[r] = cos[:, d - 8]
            rs[r] = sin[:, d - 8]
    c["rope_c"] = rc
    c["rope_s"] = rs
    c["rope_s2"] = np.ascontiguousarray(rs[prow])
    k = np.arange(128)[:, None]
    q = np.arange(128)[None, :]
    neg = np.float32(-BIG)
    causal = np.where(k > q, neg, 0).astype(np.float32)
    winfar = np.where(k <= q, neg, 0).astype(np.float32)
    c["neg_causal"] = np.tile(causal, (1, 4)).astype(bf)
    c["neg_winfar"] = np.tile(winfar, (1, 4)).astype(bf)
    strict = np.where(k >= q, neg, 0).astype(np.float32)
    nsb = np.zeros((4, 128, 512), np.float32)
    for r in range(4):
        for s in range(4):
            if s < r:
                nsb[r, :, s * 128:(s + 1) * 128] = neg
            elif s == r:
                nsb[r, :, s * 128:(s + 1) * 128] = strict
    c["neg_strict"] = strict.astype(bf)
    e_all = np.zeros((128, S_LEN), np.float32)
    for j in range(32):
        e_all[j, j * 64:(j + 1) * 64] = 1.0
    c["e_all"] = e_all.astype(bf)
    cmp_end = 16 * np.arange(128) + 31
    m = (cmp_end[:, None] <= np.arange(S_LEN)[None, :]).astype(np.float32)
    m[127] = 0
    c["mask_cmp"] = m.astype(bf)
    n_cmp, n_sel = 127, 32
    cs = np.arange(n_cmp) * 16
    ce = cs + 32
    ss = np.arange(n_sel) * 64
    se = ss + 64
    ov = np.clip(np.minimum(ce[:, None], se[None, :]) - np.maximum(cs[:, None], ss[None, :]), 0, None) / 32.0
    va = np.zeros((128, 2, 97), np.float32)
    va[:127, :, 64:96] = ov[:, None, :]
    va[:127, :, 96] = 1.0
    c["vc_aug"] = va.astype(bf)
    tm = np.zeros((128, 16, 32), np.float32)
    ta = np.zeros((128, 16, 32), np.float32)
    for qt in range(16):
        for p in range(128):
            t = qt * 128 + p
            cur = t // 64
            for j in range(32):
                forced = (j == 0) or (j == cur) or (j == cur - 1)
                valid = j * 64 <= t
                if forced:
                    ta[p, qt, j] = BIG
                elif valid:
                    tm[p, qt, j] = 1.0
                else:
                    ta[p, qt, j] = -BIG
    c["topk_m"] = tm
    c["topk_a"] = ta
    u = np.where(np.arange(128)[:, None] >= np.arange(128)[None, :], -1.0, 0.0).astype(np.float32)
    c["u_neg"] = u.astype(bf)
    c["ones_b"] = np.ones((128, 128), np.float32).astype(bf)
    return c


CONST_SPECS = [
    ("ident_b", [128, 128], BF16), ("ident_f", [128, 128], F32), ("perm_rot", [128, 128], BF16),
    ("rope_c", [128, S_LEN], F32), ("rope_s", [128, S_LEN], F32), ("rope_s2", [128, S_LEN], F32),
    ("neg_causal", [128, 512], BF16), ("neg_winfar", [128, 512], BF16), ("neg_strict", [128, 128], BF16),
    ("e_all", [128, S_LEN], BF16), ("mask_cmp", [128, S_LEN], BF16), ("vc_aug", [128, 2, 97], BF16),
    ("topk_m", [128, 16, 32], F32), ("topk_a", [128, 16, 32], F32),
    ("u_neg", [128, 128], BF16), ("ones_b", [128, 128], BF16),
]

PP_ENG = "pool"
PE_ATT_RATE = 0.52
CP_SLACK = 500.0
NSA_LOCAL = ("e_all", "mask_cmp", "topk_m", "topk_a", "neg_causal", "neg_winfar")

WEIGHT_SPECS = [
    ("w_in_r", [DEPTH, D, IN_W]), ("pos_t", [DEPTH, 2, 64, 32]), ("cmp_w1", [DEPTH, 2, 2048, 128]),
    ("cmp_b1", [DEPTH, 2, 128]), ("cmp_w2", [DEPTH, 2, 128, 64]),
    ("w_mrg", [DEPTH, D, 3072]), ("w_out", [DEPTH, D, D]), ("ln_p", [DEPTH, 4, D]),
    ("w_up_r", [DEPTH, D, 2 * D_FF]), ("conv_p", [DEPTH, 2 * D_FF, 4]), ("w_down", [DEPTH, D_FF, D]),
]


def prep_inputs(inputs):
    g = lambda k: np.asarray(inputs[k], dtype=np.float32)
    w = {}
    w["w_in_r"] = np.ascontiguousarray(g("w_in")[:, :, _win_col_index()])
    w["pos_t"] = np.ascontiguousarray(np.stack([g("cmp_pos_k"), g("cmp_pos_v")], 1).transpose(0, 1, 3, 2))
    w["cmp_w1"] = np.ascontiguousarray(np.stack([g("cmp_w1_k"), g("cmp_w1_v")], 1))
    w["cmp_b1"] = np.ascontiguousarray(np.stack([g("cmp_b1_k"), g("cmp_b1_v")], 1))
    w["cmp_w2"] = np.ascontiguousarray(np.stack([g("cmp_w2_k"), g("cmp_w2_v")], 1))
    win = g("w_in")
    wbr = np.concatenate([g("w_branch_a"), g("w_branch_b")], axis=1)
    blocks = []
    for j in range(8):
        blocks += [win[:, :, 2840 + j * 128:2840 + (j + 1) * 128], win[:, :, 3864 + j * 128:3864 + (j + 1) * 128],
                   wbr[:, :, j * 128:(j + 1) * 128]]
    w["w_mrg"] = np.ascontiguousarray(np.concatenate(blocks, axis=2))
    w["w_out"] = g("w_out")
    w["ln_p"] = np.ascontiguousarray(np.stack([g("ln_mix_g"), g("ln_mix_b"), g("ln_ffn_g"), g("ln_ffn_b")], 1))
    fi = _ffn_col_index()
    w["w_up_r"] = np.ascontiguousarray(g("w_up")[:, :, fi])
    cp = np.concatenate([g("conv_w").transpose(0, 2, 1), g("conv_b")[:, :, None]], axis=2)
    w["conv_p"] = np.ascontiguousarray(cp[:, fi, :])
    w["w_down"] = g("w_down")
    return w


class Prog:
    def __init__(self, layers, debug=(), stop=None, final_from=None):
        self.layers = list(layers)
        self.debug = debug
        self.stop = stop
        _CUR_JOIN[0] = None
        self.nc = bass.Bass("TRN2", target_bir_lowering=False)
        nc = self.nc
        self.x_in = nc.dram_tensor("x_in", [S_LEN, D], F32, kind="ExternalInput").ap()
        self.out = nc.dram_tensor("out", [S_LEN, D], F32, kind="ExternalOutput").ap()
        self.xres = nc.dram_tensor("xres", [S_LEN, D], F32).ap()
        self.cd = {n: nc.dram_tensor("c_" + n, s, dt, kind="ExternalInput").ap() for n, s, dt in CONST_SPECS}
        self.wd = {n: nc.dram_tensor(n, s, F32, kind="ExternalInput").ap() for n, s in WEIGHT_SPECS}
        self.dbg_out = {}
        self.build()

    def sb(self, st, name, shape, dt):
        self._uid = getattr(self, "_uid", 0) + 1
        return st.enter_context(self.nc.sbuf_tensor("%s_%d" % (name, self._uid), shape, dt))

    POOLS = {"a": (0, 1, 2, 3), "b": (4, 5, 6, 7), "z": (0, 1, 2, 3), "c": (4, 5), "acc": (6, 7)}

    def bank(self, pool):
        i = self.bank_i.get(pool, 0)
        self.bank_i[pool] = i + 1
        ids = self.POOLS[pool]
        k = ids[i % len(ids)]
        return self.pbank[k], self.pdep[k]

    def wslot(self):
        i = self.ring_i
        self.ring_i = i + 1
        return self.ring[i % 4], self.ringd[i % 4]

    def dump(self, name, ap, deps, shape, dt=F32):
        if name not in self.debug:
            return
        t = self.nc.dram_tensor("dbg_" + name, shape, dt, kind="ExternalOutput").ap()
        self.dbg_out[name] = "dbg_" + name
        self.S.dma("sp", t, ap, reads=deps, final=True)

    def mm(self, out, lhsT, rhs, start, stop, reads, writes):
        return self.S.op("pe", "matmul", dict(out=out, lhsT=lhsT, rhs=rhs, start=start, stop=stop), reads, writes)

    def tr(self, out, in_, ident, reads, writes):
        return self.S.op("pe", "transpose", dict(out=out, in_=in_, identity=ident), reads, writes)

    def act(self, out, in_, func, reads, writes, scale=1.0, bias=None):
        kw = dict(out=out, in_=in_, func=func, scale=scale)
        if bias is not None:
            kw["bias"] = bias
        return self.S.op("act", "activation", kw, reads, writes)

    def cp(self, eng, out, in_, reads, writes):
        if eng == "act":
            return self.S.op("act", "copy", dict(out=out, in_=in_), reads, writes)
        return self.S.op(eng, "tensor_copy", dict(out=out, in_=in_), reads, writes)

    def tt(self, eng, out, in0, in1, op, reads, writes):
        return self.S.op(eng, "tensor_tensor", dict(out=out, in0=in0, in1=in1, op=op), reads, writes)

    def ts(self, eng, out, in0, s1, s2, op0, op1, reads, writes):
        kw = dict(out=out, in0=in0, scalar1=s1, scalar2=s2, op0=op0)
        if op1 is not None:
            kw["op1"] = op1
        return self.S.op(eng, "tensor_scalar", kw, reads, writes)

    def stt(self, out, in0, scalar, in1, op0, op1, reads, writes):
        return self.S.op("dve", "scalar_tensor_tensor", dict(out=out, in0=in0, scalar=scalar, in1=in1, op0=op0, op1=op1), reads, writes)

    def memset(self, eng, ap, val, writes):
        return self.S.op(eng, "memset", dict(ap=ap, constant=val), (), writes)

    def build(self):
        nc = self.nc
        with ExitStack() as st:
            self.S = S = Sched(nc, st)
            self.xT = self.sb(st, "xT", [128, 8, S_LEN], BF16)
            self.xTd = [Dep() for _ in range(NT)]
            self.ring = [self.sb(st, "ring%d" % i, [128, 4096], BF16) for i in range(4)]
            self.ringd = [Dep() for _ in range(4)]
            self.ring_i = 0
            self.pre = {}
            self.xb_i = 0
            self.ppair = [st.enter_context(nc.psum_tensor("pp%d" % i, [128, 1024], F32)) for i in range(4)]
            self.pbank = [self.ppair[i // 2][:, (i % 2) * 512:(i % 2 + 1) * 512] for i in range(8)]
            self.pdep = [Dep() for _ in range(8)]
            self.bank_i = {}
            C = {}
            cdep = Dep()
            for n, s, dt in CONST_SPECS:
                if n in ("rope_c", "rope_s", "rope_s2") or n in NSA_LOCAL:
                    continue
                C[n] = self.sb(st, "k_" + n, s, dt)
                S.dma("sp", C[n][:], self.cd[n], writes=[cdep])
            self.C = C
            self.cdep = cdep
            self.vcaug = C["vc_aug"]
            self.vcaug_d = Dep()
            self.vcaug_d.writer = cdep.writer
            self.lnp = self.sb(st, "lnp", [128, 2, D], F32)
            self.lnp_d = Dep()
            self.lnsm = self.sb(st, "lnsm", [128, 64], F32)
            self.lnsm_d = [Dep(), Dep()]
            self.ln_i = 0
            for k in range(2):
                self.memset("pool", self.lnsm[:, k * 32 + 20:k * 32 + 21], -0.5, [self.lnsm_d[k]])
            self.xres_d = [Dep() for _ in range(NT)]

            first = True
            for li, l in enumerate(self.layers):
                last = (li == len(self.layers) - 1)
                src = self.x_in if first else self.xres
                if first:
                    self.load_xT(src)
                done = False
                with ExitStack() as sa:
                    self.oaT = self.sb(sa, "oaT", [128, 4, S_LEN], BF16)
                    self.obT = self.sb(sa, "obT", [128, 4, S_LEN], BF16)
                    self.oaT_d = [Dep() for _ in range(NT)]
                    self.obT_d = [Dep() for _ in range(NTC)]
                    self.phase_nsa(l)
                    S.soft_barrier()
                    if self.stop is not None and self.stop.startswith("nsa"):
                        done = True
                    if not done:
                        self.phase_sb(l)
                        S.soft_barrier()
                        if self.stop == "sb":
                            done = True
                    if not done:
                        self.phase_merge(l, src)
                        self.dump("xT_m%d" % l, self.xT[:], self.xTd, [128, 8, S_LEN], BF16)
                        S.soft_barrier()
                        if self.stop == "merge":
                            done = True
                if done:
                    break
                self.phase_ffn(l, self.out if last else self.xres)
                if not last:
                    ln_ = self.layers[li + 1]
                    self.prefetch(("nsa", ln_, 4), self.win_view(ln_, OFF_FM_NSA + 4 * 256, 256), (8, 256))
                    self.prefetch(("nsa", ln_, "vc"), self.win_view(ln_, OFF_FM_NSA + 1792, 408), (8, 408))
                S.soft_barrier()
                first = False
            S.emit()

    def load_xT(self, src):
        S = self.S
        with ExitStack() as st:
            xt = [self.sb(st, "ld_x%d" % i, [128, D], F32) for i in range(2)]
            self.xb = [self.sb(st, "xb", [128, D], BF16) for i in range(2)]
            self.xb_d = [Dep(), Dep()]
            xd = [Dep() for _ in range(2)]
            for tt in range(NT):
                b = tt % 2
                S.dma("sp", xt[b][:], src[tt * 128:(tt + 1) * 128, :], writes=[xd[b]])
                self.transpose_to_xT(xt[b], xd[b], tt)
            S.soft_barrier()

    def transpose_to_xT(self, xtile, xdep, tt):
        k = self.xb_i % 2
        self.xb_i += 1
        xb, xbd = self.xb[k], self.xb_d[k]
        self.cp("act", xb[:], xtile[:], [xdep], [xbd])
        pb, pd = self.bank("a")
        pv = pb[:].bitcast(BF16)
        idb = self.C["ident_b"]
        for c in range(8):
            self.tr(pv[:, c * 128:(c + 1) * 128], xb[:, c * 128:(c + 1) * 128], idb[:], [xbd, self.cdep], [pd])
        self.cp("act", self.xT[:, :, tt * 128:(tt + 1) * 128], pv.rearrange("p (c t) -> p c t", c=8), [pd], [self.xTd[tt]])

    def prefetch(self, key, src3, shape):
        self.pre[key] = self.load_w(src3, shape)

    def load_w(self, src3, shape, key=None):
        if key is not None and key in self.pre:
            return self.pre.pop(key)
        slot, dep = self.wslot()
        a, b = shape
        view = slot[:, 0:a * b].rearrange("p (a b) -> p a b", a=a)
        self.S.dma("pool", view, src3, writes=[dep])
        return view, dep

    def proj_fm(self, wv, wdep, col0, tc, pool="a"):
        pb, pd = self.bank(pool)
        rd = [wdep] + self.xTd[tc * 4:(tc + 1) * 4]
        for kc in range(8):
            self.mm(pb[:], wv[:, kc, col0:col0 + 128], self.xT[:, kc, tc * 512:(tc + 1) * 512], kc == 0, kc == 7, rd, [pd])
        return pb, pd

    def proj_tm(self, wv, wdep, col0, ncol, tt):
        pb, pd = self.bank("a")
        rd = [wdep, self.xTd[tt]]
        for kc in range(8):
            self.mm(pb[:, 0:ncol], self.xT[:, kc, tt * 128:(tt + 1) * 128], wv[:, kc, col0:col0 + ncol], kc == 0, kc == 7, rd, [pd])
        return pb, pd

    def win_view(self, l, c0, ncol):
        return self.wd["w_in_r"][l].rearrange("(c p) n -> p c n", p=128)[:, :, c0:c0 + ncol]

    def phase_nsa(self, l):
        S = self.S
        C = dict(self.C)
        with ExitStack() as st:
            for n, s_, dt in CONST_SPECS:
                if n in NSA_LOCAL:
                    C[n] = self.sb(st, "k_" + n, s_, dt)
                    S.dma("sp", C[n][:], self.cd[n], writes=[self.cdep])
            QAz = [self.sb(st, "QAz%d" % g, [128, 4, S_LEN], BF16) for g in range(2)]
            KVT = self.sb(st, "KVT", [128, 4, S_LEN], BF16)
            Vaug = self.sb(st, "Vaug", [128, NT, 4, 65], BF16)
            GA = self.sb(st, "GA", [128, NT, 24], F32)
            qa_d = [Dep() for _ in range(NTC)]
            kv_d = [Dep() for _ in range(4)]
            va_d = [Dep() for _ in range(NT)]
            ga_d = [Dep() for _ in range(NT)]
            for g in range(2):
                self.memset("dve", QAz[g][(1 - g) * 64:(2 - g) * 64, :, :], 0.0, qa_d)
            self.memset("dve", Vaug[:, :, :, 64:65], 1.0, va_d)
            KCMP = self.sb(st, "KCMP", [128, 128], BF16)
            kcmp_d = Dep()
            self.memset("dve", KCMP[:], 0.0, [kcmp_d])
            with ExitStack() as s2:
                ropeC = self.sb(s2, "ropeC", [128, S_LEN], F32)
                ropeS = self.sb(s2, "ropeS2", [128, S_LEN], F32)
                rd = Dep()
                S.dma("sp", ropeC[:], self.cd["rope_c"], writes=[rd])
                S.dma("sp", ropeS[:], self.cd["rope_s2"], writes=[rd])
                t1 = [self.sb(s2, "ru", [128, 512], BF16) for i in range(2)]
                t2 = [self.sb(s2, "rw", [128, 512], BF16) for i in range(2)]
                t1d = [Dep() for _ in range(2)]
                t2d = [Dep() for _ in range(2)]
                self.rope_it = 0

                def rope_pair(pi):
                    if ("nsa", l, pi) in self.pre:
                        wv, wdep = self.load_w(None, None, key=("nsa", l, pi))
                    else:
                        wv, wdep = self.load_w(self.win_view(l, OFF_FM_NSA + pi * 256, 128), (8, 128))
                    for tc in range(NTC):
                        pa, pad = self.proj_fm(wv, wdep, 0, tc)
                        b = self.rope_it % 2
                        self.rope_it += 1
                        sl = slice(tc * 512, (tc + 1) * 512)
                        self.tt("dve", t1[b][:], pa[:], ropeC[:, sl], ALU.mult, [pad, rd], [t1d[b]])
                        self.tt("dve", t2[b][:], pa[:], ropeS[:, sl], ALU.mult, [pad, rd], [t2d[b]])
                        po, pod = self.bank("a")
                        self.mm(po[:], C["ident_b"][:], t1[b][:], True, False, [t1d[b], self.cdep], [pod])
                        self.mm(po[:], C["perm_rot"][:], t2[b][:], False, True, [t2d[b], self.cdep], [pod])
                        if pi < 4:
                            for g in range(2):
                                r = slice(g * 64, (g + 1) * 64)
                                self.cp("act", QAz[g][r, pi, sl], po[r, :], [pod], [qa_d[tc]])
                        else:
                            self.cp("act", KVT[:, pi - 4, sl], po[:], [pod], [kv_d[pi - 4]])

                rope_pair(4)
                wv, wdep = self.load_w(self.win_view(l, OFF_FM_NSA + 1792, 408), (8, 408), key=("nsa", l, "vc"))
                for tc in range(NTC):
                    pa, pad = self.proj_fm(wv, wdep, 0, tc)
                    self.cp("act", KVT[:, 3, tc * 512:(tc + 1) * 512], pa[:], [pad], [kv_d[3]])
                for tt in range(NT):
                    pa, pad = self.proj_tm(wv, wdep, 128, 280, tt)
                    self.cp("act", Vaug[:, tt, :, 0:64], pa[:, 0:256].rearrange("p (a d) -> p a d", a=4), [pad], [va_d[tt]])
                    self.act(GA[:, tt, :], pa[:, 256:280], AF.Tanh, [pad], [ga_d[tt]], scale=0.5)
                    self.ts("dve", GA[:, tt, :], GA[:, tt, :], 0.5, 0.5, ALU.mult, ALU.add, [ga_d[tt]], [ga_d[tt]])
                posf = self.sb(s2, "posf", [128, 2, 32], F32)
                posb = self.sb(s2, "posb", [128, 2, 32], BF16)
                w2f = self.sb(s2, "w2f", [128, 2, 64], F32)
                w2b = self.sb(s2, "w2b", [128, 2, 64], BF16)
                b1 = self.sb(s2, "b1", [128, 2], F32)
                biasv = self.sb(s2, "biasv", [128, 2], F32)
                hu = [self.sb(s2, "hu", [128, 128], F32) for i in range(2)]
                hsq = [self.sb(s2, "hsq", [128, 128], F32) for i in range(2)]
                hth = [self.sb(s2, "hth", [128, 128], F32) for i in range(2)]
                hidb = [self.sb(s2, "hidb", [128, 128], BF16) for i in range(2)]
                sd = Dep()
                hds = [Dep(), Dep()]
                for kv in range(2):
                    for half in range(2):
                        S.dma("sp", posf[half * 64:(half + 1) * 64, kv, :], self.wd["pos_t"][l, kv], writes=[sd])
                    S.dma("sp", w2f[:, kv, :], self.wd["cmp_w2"][l, kv], writes=[sd])
                    S.dma("sp", b1[:, kv:kv + 1], self.wd["cmp_b1"][l, kv].rearrange("(m o) -> m o", o=1), writes=[sd])
                self.cp("dve", posb[:], posf[:], [sd], [sd])
                self.cp("dve", w2b[:], w2f[:], [sd], [sd])
                N = 127
                hi = 0
                for kv in range(2):
                    slot, wdep = self.wslot()
                    w1s = slot[:, :].rearrange("p (l m) -> p l m", l=32)
                    srcw = self.wd["cmp_w1"][l, kv].rearrange("(l d) m -> d l m", d=64)
                    for half in range(2):
                        S.dma("pool", w1s[half * 64:(half + 1) * 64, :, :], srcw, writes=[wdep])
                    pb, pd = self.bank("b")
                    for li_ in range(32):
                        self.mm(pb[:, 0:2], w1s[0:64, li_, :], posb[0:64, kv, li_:li_ + 1].broadcast_to([64, 2]),
                                li_ == 0, li_ == 31, [wdep, sd], [pd])
                    bvd = Dep()
                    self.tt("dve", biasv[:, kv:kv + 1], pb[:, 0:1], b1[:, kv:kv + 1], ALU.add, [pd, sd], [bvd])
                    for g in range(2):
                        r = slice(g * 64, (g + 1) * 64)
                        ph, phd = self.bank("b")
                        idx = 0 if kv == 0 else 3
                        k_ = hi % 2
                        hi += 1
                        hd = hds[k_]
                        for li_ in range(32):
                            self.mm(ph[:, 0:N], w1s[r, li_, :], KVT[r, idx, li_:li_ + 16 * 126 + 1:16], li_ == 0, li_ == 31, [wdep, kv_d[idx]], [phd])
                        self.ts("dve", hu[k_][:, 0:N], ph[:, 0:N], biasv[:, kv:kv + 1], None, ALU.add, None, [phd, bvd], [hd])
                        self.tt("dve", hsq[k_][:, 0:N], hu[k_][:, 0:N], hu[k_][:, 0:N], ALU.mult, [hd], [hd])
                        self.tt("dve", hsq[k_][:, 0:N], hsq[k_][:, 0:N], hu[k_][:, 0:N], ALU.mult, [hd], [hd])
                        self.stt(hsq[k_][:, 0:N], hsq[k_][:, 0:N], 0.044715, hu[k_][:, 0:N], ALU.mult, ALU.add, [hd], [hd])
                        self.act(hth[k_][:, 0:N], hsq[k_][:, 0:N], AF.Tanh, [hd], [hd], scale=0.7978845608028654)
                        self.stt(hth[k_][:, 0:N], hth[k_][:, 0:N], 1.0, hu[k_][:, 0:N], ALU.add, ALU.mult, [hd], [hd])
                        self.ts("dve", hidb[k_][:, 0:N], hth[k_][:, 0:N], 0.5, None, ALU.mult, None, [hd], [hd])
                        po, pod = self.bank("b")
                        if kv == 0:
                            self.mm(po[r, 0:N], w2b[:, 0, :], hidb[k_][:, 0:N], True, True, [hd, sd], [pod])
                            self.cp("act", KCMP[r, 0:N], po[r, 0:N], [pod], [kcmp_d])
                        else:
                            self.mm(po[0:N, 0:64], hidb[k_][:, 0:N], w2b[:, 1, :], True, True, [hd, sd], [pod])
                            self.cp("act", self.vcaug[0:N, g, 0:64], po[0:N, 0:64], [pod], [self.vcaug_d])
                for pi in (0, 1, 2, 3, 5, 6):
                    rope_pair(pi)
                S.soft_barrier()
            self.dump("qaz0", QAz[0][:], qa_d, [128, 4, S_LEN], BF16)
            self.dump("kvt", KVT[:], kv_d, [128, 4, S_LEN], BF16)
            self.dump("kcmp", KCMP[:], [kcmp_d], [128, 128], BF16)
            self.dump("vcaug", self.vcaug[:], [self.vcaug_d], [128, 2, 97], BF16)
            if self.stop in ("nsa_proj", "nsa_cmp"):
                return
            PT = [self.sb(st, "PT", [128, 512], BF16) for i in range(4)]
            ptd = [Dep() for _ in range(4)]
            self.pt_i = 0
            NP = 3
            oacc = [self.sb(st, "oacc", [128, 8, 64], F32) for i in range(NP)]
            oaccd = [[Dep(), Dep()] for _ in range(NP)]
            tmpc = [self.sb(st, "tmpc", [128, 4, 64], F32) for i in range(2)]
            tmpcd = [Dep() for _ in range(2)]
            self.tci = 0
            sm = [self.sb(st, "sm", [128, 64], F32) for i in range(NP)]
            smd = [[Dep() for _ in range(4)] for _ in range(NP)]
            sc = [self.sb(st, "sc", [128, 32], F32) for i in range(NP)]
            sc2 = [self.sb(st, "sc2", [128, 32], F32) for i in range(NP)]
            negsel = [self.sb(st, "negsel", [128, 32], BF16) for i in range(NP)]
            negselT = [self.sb(st, "negselT", [128, 128], BF16) for i in range(NP)]
            nsd = [Dep() for _ in range(NP)]
            scd = [Dep() for _ in range(NP)]
            for i in range(NP):
                self.memset("dve", negselT[i][:], 0.0, [nsd[i]])

            def evac(accb, accd, W, br, first, qt, g, par):
                ob = qt % NP
                b0 = 16 * br
                sm_ = sm[par]
                sd_ = smd[par][br]
                v = accb[:, 0:4 * W].rearrange("p (h w) -> p h w", h=4)
                self.ts("dve", sm_[:, b0:b0 + 4], v[:, :, W - 1], 1e-30, None, ALU.max, None, [accd], [sd_])
                S.op("dve", "reciprocal", dict(out=sm_[:, b0 + 4:b0 + 8], in_=sm_[:, b0:b0 + 4]), [sd_], [sd_])
                gcol = slice(4 * g * 3 + br, 4 * g * 3 + br + 10, 3)
                self.tt("dve", sm_[:, b0 + 8:b0 + 12], sm_[:, b0 + 4:b0 + 8], GA[:, qt, gcol], ALU.mult, [sd_, ga_d[qt]], [sd_])
                coef = sm_[:, b0 + 8:b0 + 12].unsqueeze(2).broadcast_to([128, 4, 64])
                dst = oacc[ob][:, g * 4:(g + 1) * 4, :]
                od_ = oaccd[ob][g]
                if first:
                    self.tt("dve", dst, v[:, :, 0:64], coef, ALU.mult, [accd, sd_], [od_])
                else:
                    tb = self.tci % 2
                    self.tci += 1
                    self.tt("dve", tmpc[tb][:], v[:, :, 0:64], coef, ALU.mult, [accd, sd_], [tmpcd[tb]])
                    self.tt("dve", dst, dst, tmpc[tb][:], ALU.add, [tmpcd[tb], od_], [od_])
                return v

            def cmp_topk(qt, g, par):
                qs = slice(qt * 128, (qt + 1) * 128)
                qrhs = QAz[g][:, :, qs]
                qdeps = [qa_d[qt // 4]]
                sbk, sbd = self.bank("a")
                self.mm(sbk[0:127, :], KCMP[:, 0:127], qrhs, True, True, qdeps + [kcmp_d], [sbd])
                pi_ = self.pt_i % 4
                self.pt_i += 1
                self.act(PT[pi_][0:127, :], sbk[0:127, :], AF.Exp, [sbd], [ptd[pi_]], scale=0.125)
                ptv = PT[pi_][0:127, :].rearrange("p (h q) -> p h q", h=4)
                self.tt("dve", ptv, ptv, C["mask_cmp"][0:127, qs].unsqueeze(1).broadcast_to([127, 4, 128]), ALU.mult,
                        [ptd[pi_], self.cdep], [ptd[pi_]])
                accb, accd = self.bank("b")
                for h in range(4):
                    self.mm(accb[:, h * 97:(h + 1) * 97], PT[pi_][0:127, h * 128:(h + 1) * 128], self.vcaug[0:127, g, :],
                            h == 0, h == 3, [ptd[pi_], self.vcaug_d], [accd])
                v = evac(accb, accd, 97, 0, True, qt, g, par)
                if qt < 8:
                    return
                sm_, sc_, sd_ = sm[par], sc[par], scd[par]
                s0 = smd[par][0]
                s3 = smd[par][3]
                self.ts("dve", sc_[:], v[:, 0, 64:96], sm_[:, 4:5], None, ALU.mult, None, [accd, s0], [sd_])
                for h in range(1, 4):
                    self.stt(sc_[:], v[:, h, 64:96], sm_[:, 4 + h:5 + h], sc_[:], ALU.mult, ALU.add, [accd, s0, sd_], [sd_])
                self.tt("dve", sc_[:], sc_[:], C["topk_m"][:, qt, :], ALU.mult, [sd_, self.cdep], [sd_])
                self.tt("dve", sc_[:], sc_[:], C["topk_a"][:, qt, :], ALU.add, [sd_, self.cdep], [sd_])
                S.op("dve", "max", dict(out=sm_[:, 48:56], in_=sc_[:]), [sd_], [s3])
                S.op("dve", "match_replace", dict(out=sc2[par][:], in_to_replace=sm_[:, 48:56], in_values=sc_[:], imm_value=-3.0e4),
                     [sd_, s3], [sd_])
                S.op("dve", "max", dict(out=sm_[:, 56:64], in_=sc2[par][:]), [sd_], [s3])
                self.ts("dve", negsel[par][:], sc_[:], sm_[:, 63:64], -BIG, ALU.is_lt, ALU.mult, [sd_, s3], [sd_])
                tb_, tbd = self.bank("a")
                tbv = tb_[:].bitcast(BF16)
                self.tr(tbv[0:32, 0:128], negsel[par][:], C["ident_b"][:], [sd_, self.cdep], [tbd])
                self.cp("act", negselT[par][0:32, :], tbv[0:32, 0:128], [tbd], [nsd[par]])

            def selwin(qt, g, par):
                qs = slice(qt * 128, (qt + 1) * 128)
                qrhs = QAz[g][:, :, qs]
                qdeps = [qa_d[qt // 4]]
                need_sel = qt >= 8
                for br, kidx, vtype in ((2, 2, 1), (1, 1, 0)):
                    kts = list(range(0, qt + 1)) if br == 1 else list(range(max(0, qt - 4), qt + 1))
                    accb, accd = self.bank("b")
                    for ki, kt in enumerate(kts):
                        ks_ = slice(kt * 128, (kt + 1) * 128)
                        extra = []
                        if br == 1 and need_sel:
                            extra.append((C["e_all"][:, ks_], negselT[par][:, :].unsqueeze(1).broadcast_to([128, 4, 128]), [nsd[par], self.cdep]))
                        if kt == qt:
                            extra.append((C["ident_b"][:], C["neg_causal"][:], [self.cdep]))
                        if br == 2 and kt == qt - 4:
                            extra.append((C["ident_b"][:], C["neg_winfar"][:], [self.cdep]))
                        sbk, sbd = self.bank("a")
                        self.mm(sbk[:], KVT[:, kidx, ks_], qrhs, True, len(extra) == 0, qdeps + [kv_d[kidx]], [sbd])
                        for xi, (lh, rh, dd) in enumerate(extra):
                            self.mm(sbk[:], lh, rh, False, xi == len(extra) - 1, dd, [sbd])
                        pi_ = self.pt_i % 4
                        self.pt_i += 1
                        self.act(PT[pi_][:], sbk[:], AF.Exp, [sbd], [ptd[pi_]], scale=0.125)
                        for h in range(4):
                            self.mm(accb[:, h * 65:(h + 1) * 65], PT[pi_][:, h * 128:(h + 1) * 128], Vaug[:, kt, vtype * 2 + g, :],
                                    ki == 0 and h == 0, ki == len(kts) - 1, [ptd[pi_], va_d[kt]], [accd])
                    evac(accb, accd, 65, br, False, qt, g, par)

            S.pe_rate = PE_ATT_RATE
            steps = [(qt, g) for qt in range(NT) for g in range(2)]
            for i in range(2):
                cmp_topk(steps[i][0], steps[i][1], i % NP)
            for i, (qt, g) in enumerate(steps):
                if i + 2 < len(steps):
                    cmp_topk(steps[i + 2][0], steps[i + 2][1], (i + 2) % NP)
                selwin(qt, g, i % NP)
                if g == 1:
                    ob = qt % NP
                    qs = slice(qt * 128, (qt + 1) * 128)
                    tb_, tbd = self.bank("a")
                    for c in range(4):
                        self.tr(tb_[:, c * 128:(c + 1) * 128], oacc[ob][:, 2 * c:2 * c + 2, :].rearrange("p a d -> p (a d)"),
                                C["ident_f"][:], oaccd[ob] + [self.cdep], [tbd])
                    self.cp("act", self.oaT[:, :, qs], tb_[:].rearrange("p (c t) -> p c t", c=4), [tbd], [self.oaT_d[qt]])
            S.pe_rate = 0.42
            self.dump("oaT", self.oaT[:], self.oaT_d, [128, 4, S_LEN], BF16)
            self.prefetch(("sbq", l), self.win_view(l, OFF_FM_SB, 512), (8, 512))
            self.prefetch(("sbk", l), self.win_view(l, OFF_FM_SB + 512, 512), (8, 512))

    def phase_sb(self, l):
        S = self.S
        C = self.C
        with ExitStack() as st:
            QBz = [self.sb(st, "QBz", [128, 4, S_LEN], BF16) for g in range(2)]
            KB = self.sb(st, "KB", [128, 4, S_LEN], BF16)
            VB = self.sb(st, "VB", [128, NT, 512], BF16)
            qb_d = [Dep() for _ in range(NTC)]
            kb_d = Dep()
            vb_d = [Dep() for _ in range(NT)]
            for par in range(2):
                self.memset("dve", QBz[par][(1 - par) * 64:(2 - par) * 64, :, :], 0.0, qb_d)
            wq, wqd = self.load_w(self.win_view(l, OFF_FM_SB, 512), (8, 512), key=("sbq", l))
            wk, wkd = self.load_w(self.win_view(l, OFF_FM_SB + 512, 512), (8, 512), key=("sbk", l))
            for c in range(4):
                for tc in range(NTC):
                    sl = slice(tc * 512, (tc + 1) * 512)
                    pa, pad = self.proj_fm(wq, wqd, c * 128, tc)
                    for par in range(2):
                        r = slice(par * 64, (par + 1) * 64)
                        self.act(QBz[par][r, c, sl], pa[r, :], AF.Copy, [pad], [qb_d[tc]], scale=0.125)
                    pa, pad = self.proj_fm(wk, wkd, c * 128, tc)
                    self.cp("act", KB[:, c, sl], pa[:], [pad], [kb_d])
            wv, wvd = self.load_w(self.win_view(l, OFF_TM_SB, 512), (8, 512))
            for tt in range(NT):
                pa, pad = self.proj_tm(wv, wvd, 0, 512, tt)
                self.cp("act", VB[:, tt, :], pa[:], [pad], [vb_d[tt]])
            self.dump("qbz0", QBz[0][:], qb_d, [128, 4, S_LEN], BF16)
            self.dump("kb", KB[:], [kb_d], [128, 4, S_LEN], BF16)
            E2 = [self.sb(st, "sbE", [128, 2, 512], F32) for i in range(2)]
            Lm2 = [self.sb(st, "sbL", [128, 2, 512], BF16) for i in range(2)]
            tmp2 = [self.sb(st, "sbT", [128, 2, 512], F32) for i in range(2)]
            A2 = [self.sb(st, "sbA", [128, 2, 512], BF16) for i in range(2)]
            carry2 = self.sb(st, "sbC", [128, 2, 512], F32)
            opair = [self.sb(st, "sbO", [128, 4, 128], F32) for i in range(2)]
            Ed = [Dep() for _ in range(2)]
            Ld = [Dep() for _ in range(2)]
            Td = [Dep() for _ in range(2)]
            Ad = [Dep() for _ in range(2)]
            cd_ = Dep()
            od = [Dep() for _ in range(2)]
            oi = 0
            ui = 0
            S.pe_rate = PE_ATT_RATE
            for c in range(4):
                for qc in range(NTC):
                    qsl = slice(qc * 512, (qc + 1) * 512)
                    ob = oi % 2
                    oi += 1
                    ktop = 4 * qc + 3
                    accs = [(self.pbank[6 + par], self.pdep[6 + par]) for par in range(2)]
                    first_pv = [True, True]
                    for kt in range(ktop, -1, -1):
                        b = ui % 2
                        ui += 1
                        r = kt - 4 * qc
                        c0 = max(r, 0) * 128
                        cs = slice(c0, 512)
                        ks_ = slice(kt * 128, (kt + 1) * 128)
                        zp = self.ppair[b]
                        zds = [self.pdep[2 * b], self.pdep[2 * b + 1]]
                        zv = zp[:, :].rearrange("p (s n) -> p s n", s=2)[:, :, cs]
                        for par in range(2):
                            zb = self.pbank[2 * b + par]
                            self.mm(zb[:, cs], KB[:, c, ks_], QBz[par][:, c, qc * 512 + c0:(qc + 1) * 512], True, False, [kb_d, qb_d[qc]], [zds[par]])
                            if r >= 0:
                                self.mm(zb[:, c0:c0 + 128], C["ident_b"][:], C["neg_strict"][:], False, False, [self.cdep], [zds[par]])
                        self.act(E2[b][:, :, cs], zv, AF.Exp, zds, [Ed[b]])
                        self.act(Lm2[b][:, :, cs], E2[b][:, :, cs], AF.Ln, [Ed[b]], [Ld[b]], bias=1.0)
                        for par in range(2):
                            zb = self.pbank[2 * b + par]
                            self.mm(zb[:, cs], C["u_neg"][:], Lm2[b][:, par, cs], False, True, [Ld[b], self.cdep], [zds[par]])
                        if kt < ktop:
                            self.tt("dve", tmp2[b][:, :, cs], zv, carry2[:, :, cs], ALU.subtract, zds + [cd_], [Td[b]])
                            self.act(A2[b][:, :, cs], tmp2[b][:, :, cs], AF.Exp, [Td[b]], [Ad[b]])
                        else:
                            self.act(A2[b][:, :, cs], zv, AF.Exp, zds, [Ad[b]])
                        if kt > 0:
                            cp_ = self.ppair[2]
                            cds = [self.pdep[4], self.pdep[5]]
                            cv = cp_[:, :].rearrange("p (s n) -> p s n", s=2)[:, :, cs]
                            for par in range(2):
                                self.mm(self.pbank[4 + par][:, cs], C["ones_b"][:], Lm2[b][:, par, cs], True, True, [Ld[b], self.cdep], [cds[par]])
                            if kt == ktop:
                                self.memset("dve", carry2[:, :, 0:c0], 0.0, [cd_])
                                self.cp("dve", carry2[:, :, cs], cv, cds, [cd_])
                            else:
                                self.tt("dve", carry2[:, :, cs], carry2[:, :, cs], cv, ALU.add, cds + [cd_], [cd_])
                        for par in range(2):
                            h = 2 * c + par
                            accb, accd = accs[par]
                            for s in range(4):
                                if s < r:
                                    continue
                                self.mm(accb[:, s * 64:(s + 1) * 64], A2[b][:, par, s * 128:(s + 1) * 128], VB[:, kt, h * 64:(h + 1) * 64],
                                        first_pv[par], kt == 0, [Ad[b], vb_d[kt]], [accd])
                                first_pv[par] = False
                    for par in range(2):
                        accb, accd = accs[par]
                        self.cp("act", opair[ob][:, :, par * 64:(par + 1) * 64], accb[:, 0:256].rearrange("p (s d) -> p s d", s=4), [accd], [od[ob]])
                    tb_, tbd = self.bank("c")
                    for s in range(4):
                        self.tr(tb_[:, s * 128:(s + 1) * 128], opair[ob][:, s, :], C["ident_f"][:], [od[ob], self.cdep], [tbd])
                    self.cp("act", self.obT[:, c, qsl], tb_[:], [tbd], [self.obT_d[qc]])
            S.pe_rate = 0.42
            self.dump("obT", self.obT[:], self.obT_d, [128, 4, S_LEN], BF16)
            wmrg_ = self.wd["w_mrg"][l].rearrange("(c p) n -> p c n", p=128)
            for j in range(2):
                self.prefetch(("mrg", l, 0, j), wmrg_[:, :, j * 384:(j + 1) * 384], (8, 384))

    def ln_tile(self, ybanks, yscale, xr, xrd, tt, dst):
        S = self.S
        k = self.ln_i % 2
        self.ln_i += 1
        sm = self.lnsm[:, k * 32:(k + 1) * 32]
        smd = self.lnsm_d[k]
        r, rdep = xr, xrd
        S.op("act", "mul", dict(out=xr[:], in_=xr[:], mul=ALPHA), [xrd], [xrd])
        for half in range(2):
            yb, yd = ybanks[half]
            hs = slice(half * 512, (half + 1) * 512)
            self.stt(r[:, hs], yb[:], yscale, xr[:, hs], ALU.mult, ALU.add, [yd, xrd], [rdep])
        for half in range(2):
            hs = slice(half * 512, (half + 1) * 512)
            S.op("dve", "bn_stats", dict(out=sm[:, half * 6:(half + 1) * 6], in_=r[:, hs]), [rdep], [smd])
        S.op("dve", "bn_aggr", dict(out=sm[:, 12:14], in_=sm[:, 0:12]), [smd], [smd])
        self.ts("pool", sm[:, 14:15], sm[:, 13:14], EPS, None, ALU.add, None, [smd], [smd])
        self.tt("pool", sm[:, 15:16], sm[:, 14:15], sm[:, 20:21], ALU.pow, [smd], [smd])
        self.ts("dve", sm[:, 16:17], sm[:, 12:13], -1.0, sm[:, 15:16], ALU.mult, ALU.mult, [smd], [smd])
        self.act(r[:], r[:], AF.Identity, [rdep, smd], [rdep], scale=sm[:, 15:16], bias=sm[:, 16:17])
        self.tt("dve", r[:], r[:], self.lnp[:, 0, :], ALU.mult, [rdep, self.lnp_d], [rdep])
        self.tt("dve", r[:], r[:], self.lnp[:, 1, :], ALU.add, [rdep, self.lnp_d], [rdep])
        S.dma("sp", dst[tt * 128:(tt + 1) * 128, :], r[:], reads=[rdep], writes=[self.xres_d[tt]], final=(dst is self.out))
        self.transpose_to_xT(r, rdep, tt)

    def load_lnp(self, l, which):
        for i in range(2):
            self.S.dma("sp", self.lnp[:, i, :], self.wd["ln_p"][l, which * 2 + i].partition_broadcast(128), writes=[self.lnp_d])

    def phase_merge(self, l, src):
        S = self.S
        with ExitStack() as st:
            mT = [self.sb(st, "mT", [128, 8, 512], BF16) for i in range(2)]
            self.xb = [self.sb(st, "xb", [128, D], BF16) for i in range(2)]
            self.xb_d = [Dep(), Dep()]
            mT_d = [Dep() for _ in range(2)]
            wout = self.sb(st, "wout", [128, 8, D], BF16)
            wout_d = Dep()
            wo_src = self.wd["w_out"][l].rearrange("(c p) n -> p c n", p=128)
            for half in range(2):
                S.dma("pool", wout[:, :, half * 512:(half + 1) * 512], wo_src[:, :, half * 512:(half + 1) * 512], writes=[wout_d])
            self.load_lnp(l, 0)
            sg = [self.sb(st, "sg", [128, 512], F32) for i in range(4)]
            sgd = [Dep() for _ in range(4)]
            t12 = [self.sb(st, "t12", [128, 512], F32) for i in range(4)]
            t12d = [Dep() for _ in range(4)]
            xr = [self.sb(st, "xr", [128, D], F32) for i in range(3)]
            xrd = [Dep() for _ in range(3)]
            it = 0
            wmrg = self.wd["w_mrg"][l].rearrange("(c p) n -> p c n", p=128)
            for tc in range(NTC):
                sl = slice(tc * 512, (tc + 1) * 512)
                mb = tc % 2
                for j in range(8):
                    wv, gd = self.load_w(wmrg[:, :, j * 384:(j + 1) * 384], (8, 384), key=("mrg", l, tc, j))
                    gv = wv[:, :, 0:256]
                    bv = wv[:, :, 256:384]
                    k0 = (it % 2) * 2
                    it += 1
                    res = []
                    for i, oT, od_ in ((0, self.oaT, self.oaT_d[tc * 4:(tc + 1) * 4]), (1, self.obT, [self.obT_d[tc]])):
                        pg, pgd = self.proj_fm(gv, gd, i * 128, tc)
                        pA, pAd = self.bank("b")
                        for kc in range(4):
                            self.mm(pA[:], bv[:, i * 4 + kc, :], oT[:, kc, sl], kc == 0, kc == 3, [gd] + od_, [pAd])
                        k = k0 + i
                        self.act(sg[k][:], pg[:], AF.Tanh, [pgd], [sgd[k]], scale=0.5)
                        self.stt(t12[k][:], sg[k][:], 1.0, pA[:], ALU.add, ALU.mult, [sgd[k], pAd], [t12d[k]])
                        res.append(k)
                    self.tt("dve", mT[mb][:, j, :], t12[res[0]][:], t12[res[1]][:], ALU.add, [t12d[res[0]], t12d[res[1]]], [mT_d[mb]])
                for tt in range(tc * 4, tc * 4 + 4):
                    b = tt % 3
                    ts_ = slice(tt * 128, (tt + 1) * 128)
                    tl = slice((tt % 4) * 128, (tt % 4 + 1) * 128)
                    S.dma("sp", xr[b][:], src[ts_, :], reads=[self.xres_d[tt]], writes=[xrd[b]])
                    yb = []
                    for half in range(2):
                        pb, pd = self.bank("b")
                        for j in range(8):
                            self.mm(pb[:], mT[mb][:, j, tl], wout[:, j, half * 512:(half + 1) * 512], j == 0, j == 7, [mT_d[mb], wout_d], [pd])
                        yb.append((pb, pd))
                    self.ln_tile(yb, 0.5, xr[b], xrd[b], tt, self.xres)
            wup_ = self.wd["w_up_r"][l].rearrange("(c p) n -> p c n", p=128)
            for jp in range(2):
                self.prefetch(("up", l, jp), wup_[:, :, jp * 512:(jp + 1) * 512], (8, 512))

    def phase_ffn(self, l, dst):
        S = self.S
        with ExitStack() as st:
            gT = self.sb(st, "gT", [128, NJ, S_LEN], BF16)
            gT_d = [Dep() for _ in range(NTC)]
            convp = self.sb(st, "convp", [128, 2 * NJ, 4], F32)
            cvd = Dep()
            S.dma("sp", convp[:], self.wd["conv_p"][l].rearrange("(j p) k -> p j k", p=128), writes=[cvd])
            wup = self.wd["w_up_r"][l].rearrange("(c p) n -> p c n", p=128)
            with ExitStack() as s2:
                u = [[self.sb(s2, "u", [128, 514], F32) for b in range(2)] for i in range(2)]
                ud = [[Dep() for b in range(2)] for i in range(2)]
                acc = [[self.sb(s2, "acc", [128, 512], F32) for b in range(4)] for i in range(2)]
                accd = [[Dep() for b in range(4)] for i in range(2)]
                th = [self.sb(s2, "th", [128, 512], F32) for b in range(4)]
                thd = [Dep() for b in range(4)]
                pp = [self.sb(s2, "pp", [128, 512], F32) for b in range(4)]
                ppd = [Dep() for b in range(4)]
                ai = 0
                for jp in range(NJ // 2):
                    wv, wdep = self.load_w(wup[:, :, jp * 512:(jp + 1) * 512], (8, 512), key=("up", l, jp))
                    for jj in range(2):
                        j = 2 * jp + jj
                        for tc in range(NTC):
                            b = tc % 2
                            ab = ai % 4
                            ai += 1
                            sl = slice(tc * 512, (tc + 1) * 512)
                            for i in range(2):
                                ch = 2 * j + i
                                pu, pud = self.proj_fm(wv, wdep, jj * 256 + i * 128, tc, pool=("a" if i == 0 else "b"))
                                if tc == 0:
                                    self.memset("pool", u[i][b][:, 0:2], 0.0, [ud[i][b]])
                                else:
                                    self.cp("act", u[i][b][:, 0:2], u[i][1 - b][:, 512:514], [ud[i][1 - b]], [ud[i][b]])
                                self.cp("act", u[i][b][:, 2:514], pu[:], [pud], [ud[i][b]])
                                self.act(acc[i][ab][:], pu[:], AF.Identity, [pud, cvd], [accd[i][ab]], scale=convp[:, ch, 2:3], bias=convp[:, ch, 3:4])
                                self.stt(acc[i][ab][:], u[i][b][:, 1:513], convp[:, ch, 1:2], acc[i][ab][:], ALU.mult, ALU.add,
                                         [ud[i][b], accd[i][ab], cvd], [accd[i][ab]])
                                self.stt(acc[i][ab][:], u[i][b][:, 0:512], convp[:, ch, 0:1], acc[i][ab][:], ALU.mult, ALU.add,
                                         [ud[i][b], accd[i][ab], cvd], [accd[i][ab]])
                            self.act(th[ab][:], acc[0][ab][:], AF.Silu, [accd[0][ab]], [thd[ab]])
                            self.tt("dve", gT[:, j, sl], th[ab][:], acc[1][ab][:], ALU.mult, [thd[ab], accd[1][ab]], [gT_d[tc]])
                wdn = self.wd["w_down"][l].rearrange("(j p) n -> p j n", p=128)
                for si in range(4):
                    v = self.ring[si][:, :].rearrange("p (a b) -> p a b", a=4)
                    S.dma("pool", v, wdn[:, si * 4:(si + 1) * 4, :], writes=[self.ringd[si]])
                S.soft_barrier()
            self.dump("gT", gT[:], gT_d, [128, NJ, S_LEN], BF16)
            wdx = self.sb(st, "wdx", [128, 6, D], BF16)
            self.xb = [self.sb(st, "xb", [128, D], BF16) for i in range(2)]
            self.xb_d = [Dep(), Dep()]
            wdx_d = Dep()
            S.dma("pool", wdx[:], wdn[:, 16:22, :], writes=[wdx_d])
            self.load_lnp(l, 1)
            xr = [self.sb(st, "xr2", [128, D], F32) for i in range(3)]
            xrd = [Dep() for _ in range(3)]
            for tt in range(NT):
                b = tt % 3
                ts_ = slice(tt * 128, (tt + 1) * 128)
                S.dma("sp", xr[b][:], self.xres[ts_, :], reads=[self.xres_d[tt]], writes=[xrd[b]])
                yb = []
                for half in range(2):
                    pb, pd = self.bank("b")
                    hs = slice(half * 512, (half + 1) * 512)
                    for j in range(NJ):
                        if j < 16:
                            wj = self.ring[j // 4][:, (j % 4) * 1024:(j % 4 + 1) * 1024]
                            wjd = self.ringd[j // 4]
                        else:
                            wj = wdx[:, j - 16, :]
                            wjd = wdx_d
                        self.mm(pb[:], gT[:, j, ts_], wj[:, hs], j == 0, j == NJ - 1, [gT_d[tt // 4], wjd], [pd])
                    yb.append((pb, pd))
                self.ln_tile(yb, 1.0, xr[b], xrd[b], tt, dst)


_PROG_CACHE = {}


def _get_prog(layers, debug=(), stop=None):
    key = (tuple(layers), tuple(debug), stop)
    if key not in _PROG_CACHE:
        _PROG_CACHE[key] = Prog(layers, debug, stop)
    return _PROG_CACHE[key]


def kernel(**inputs):
    x = np.asarray(inputs["x"], dtype=np.float32)
    w = prep_inputs(inputs)
    consts = _constants()
    prog = _get_prog(range(DEPTH))
    in_maps = []
    for b in range(8):
        m = {"x_in": np.ascontiguousarray(x[b])}
        for n, _, _ in CONST_SPECS:
            m["c_" + n] = consts[n]
        m.update(w)
        in_maps.append(m)
    res = run_bass_kernel_spmd(prog.nc, in_maps, core_ids=list(range(8)))
    return np.stack([np.asarray(res.results[b]["out"], dtype=np.float32) for b in range(8)], axis=0)
```

```python
import numpy as np
from contextlib import ExitStack
import ml_dtypes
import concourse.bass as bass
import concourse.mybir as mybir
from concourse.bass_utils import run_bass_kernel_spmd

F32 = mybir.dt.float32
BF16 = mybir.dt.bfloat16
AF = mybir.ActivationFunctionType
ALU = mybir.AluOpType

S_LEN = 2048
D = 1024
NT = 16
NTC = 4
DEPTH = 4
D_FF = 2816
NJ = 22
ALPHA = float((2.0 * DEPTH) ** 0.25)
EPS = 1e-5
BIG = 1.0e4
IN_W = 3736
OFF_FM_NSA = 0
OFF_TM_NSA = 1920
OFF_FM_SB = 2200
OFF_TM_SB = 3224
OFF_GM = 3736


_CUR_JOIN = [None]


class Dep:
    __slots__ = ("writer", "readers")

    def __init__(self):
        self.writer = _CUR_JOIN[0]
        self.readers = []


def _nfree(ap):
    n = 1
    for d in ap.shape[1:]:
        n *= d
    return n


class Sched:
    ENG = ("pe", "act", "dve", "pool", "sp")
    N_DMA_SEM = 24
    XLAT = 150.0

    def __init__(self, nc, stack, reorder=True):
        self.nc = nc
        self.reorder = reorder
        self.ops = {e: [] for e in self.ENG}
        self.count = {e: 0 for e in self.ENG}
        self.known = {e: {} for e in self.ENG}
        self.semh = {}
        for e in self.ENG:
            self.semh[e] = stack.enter_context(nc.semaphore("tl_" + e))
        self.dma_i = {"sp": 0, "pool": 0}
        self.dma_last = {}
        for q in ("sp", "pool"):
            for k in range(self.N_DMA_SEM):
                self.semh[("dma", q, k)] = stack.enter_context(nc.semaphore("dma_%s_%d" % (q, k)))
        self.final_toks = []
        self.pe_rate = 0.42
        self.nodes = []
        self.seg_base = 0
        self.total = 0

    def _preds(self, reads, writes):
        base = self.seg_base
        p = set()
        for d in reads:
            if d.writer is not None and d.writer >= base:
                p.add(d.writer - base)
        for d in writes:
            if d.writer is not None and d.writer >= base:
                p.add(d.writer - base)
            for r in d.readers:
                if r >= base:
                    p.add(r - base)
        return p

    def _mark(self, gid, reads, writes):
        for d in reads:
            d.readers.append(gid)
        for d in writes:
            d.writer = gid
            d.readers = []

    def _cost(self, eng, meth, kw):
        if eng == "pe":
            if meth == "matmul":
                return 25.0 + self.pe_rate * max(_nfree(kw["rhs"]), 64)
            return 110.0
        ap = kw.get("out", kw.get("ap"))
        n = _nfree(ap) if ap is not None else 64
        if eng == "act":
            if meth == "copy":
                return 150.0 + 0.8 * n
            return 200.0 + 0.85 * n
        if eng == "dve":
            return 150.0 + 1.07 * n
        return 300.0 + 2.0 * n

    def op(self, eng, meth, kw, reads=(), writes=()):
        p = self._preds(reads, writes)
        gid = self.seg_base + len(self.nodes)
        self.nodes.append([eng, meth, kw, p, self._cost(eng, meth, kw), False, False])
        self._mark(gid, reads, writes)
        return gid

    def dma(self, q, out, in_, reads=(), writes=(), final=False):
        p = self._preds(reads, writes)
        gid = self.seg_base + len(self.nodes)
        nbytes = 1
        for d in out.shape:
            nbytes *= d
        nbytes *= 4
        lat = 2500.0 + nbytes / 150.0
        self.nodes.append([q, "dma_start", dict(out=out, in_=in_), p, lat, True, final])
        self._mark(gid, reads, writes)
        return gid

    def _order(self):
        nodes = self.nodes
        n = len(nodes)
        if not self.reorder:
            return list(range(n))
        succ = [[] for _ in range(n)]
        npend = [0] * n
        for i, nd in enumerate(nodes):
            npend[i] = len(nd[3])
            for p in nd[3]:
                succ[p].append(i)
        cpl = [0.0] * n
        for i in range(n - 1, -1, -1):
            m = 0.0
            for s_ in succ[i]:
                if cpl[s_] > m:
                    m = cpl[s_]
            cpl[i] = m + nodes[i][4]
        rtime = [0.0] * n
        finish = [0.0] * n
        ready = {e: [] for e in self.ENG}
        for i, nd in enumerate(nodes):
            if npend[i] == 0:
                ready[nd[0]].append(i)
        free_at = {e: 0.0 for e in self.ENG}
        order = []
        done = 0
        while done < n:
            best = None
            for e in self.ENG:
                rl = ready[e]
                if not rl:
                    continue
                fa = free_at[e]
                cand = None
                cand_t = None
                for i in rl:
                    t = rtime[i]
                    if t <= fa:
                        if cand is None or cand_t > fa or (cpl[i] > cpl[cand] + CP_SLACK) or (cpl[i] >= cpl[cand] - CP_SLACK and i < cand):
                            cand, cand_t = i, t
                    elif cand is None or (cand_t > fa and (t < cand_t or (t == cand_t and i < cand))):
                        cand, cand_t = i, t
                st = fa if cand_t <= fa else cand_t
                if best is None or st < best[0] or (st == best[0] and cand < best[2]):
                    best = (st, e, cand)
            assert best is not None, "scheduler deadlock"
            st, e, i = best
            ready[e].remove(i)
            nd = nodes[i]
            if nd[5]:
                free_at[e] = st + (60.0 if e == "sp" else 900.0)
                fin = st + nd[4]
            else:
                fin = st + nd[4]
                free_at[e] = fin
            finish[i] = fin
            order.append(i)
            done += 1
            for s_ in succ[i]:
                if nodes[s_][0] == e and not nd[5]:
                    lat = 0.0 if e == "pe" else 120.0
                else:
                    lat = self.XLAT
                if fin + lat > rtime[s_]:
                    rtime[s_] = fin + lat
                npend[s_] -= 1
                if npend[s_] == 0:
                    ready[nodes[s_][0]].append(s_)
        return order

    def flush(self):
        nodes = self.nodes
        if not nodes:
            return
        order = self._order()
        tok = [None] * len(nodes)
        for i in order:
            eng, meth, kw, preds, cost, is_dma, final = nodes[i]
            toks = {}
            for p in preds:
                t = tok[p]
                assert t is not None, "scheduler order violates a dependency"
                if toks.get(t[0], 0) < t[1]:
                    toks[t[0]] = t[1]
            waits = []
            kn = self.known[eng]
            for sk, v in toks.items():
                if sk == eng and eng == "pe":
                    continue
                if kn.get(sk, 0) >= v:
                    continue
                kn[sk] = v
                waits.append((sk, v))
            if is_dma:
                q = eng
                di = self.dma_i[q]
                self.dma_i[q] = di + 1
                sk = ("dma", q, di % self.N_DMA_SEM)
                val = 16 * (di // self.N_DMA_SEM + 1)
                if val > 16 and kn.get(sk, 0) < val - 16:
                    kn[sk] = val - 16
                    waits.append((sk, val - 16))
                tok[i] = (sk, val)
                self.dma_last[sk] = val
                self.ops[q].append((waits, (meth, kw), (sk, 16), None))
                if final:
                    self.final_toks.append(tok[i])
            else:
                c = self.count[eng] + 1
                self.count[eng] = c
                tok[i] = (eng, c)
                self.ops[eng].append((waits, (meth, kw), (eng, 1), c))
        self.seg_base += len(nodes)
        self.nodes = []

    def soft_barrier(self):
        base = self.seg_base
        start = getattr(self, "join_start", 0)
        preds = set(range(max(start - base, 0), len(self.nodes)))
        gid = base + len(self.nodes)
        self.nodes.append(["sp", "nop", {}, preds, 50.0, False, False])
        self.join_start = gid
        _CUR_JOIN[0] = gid

    def barrier(self):
        self.flush()
        toks = [(e, self.count[e]) for e in self.ENG if self.count[e] > 0]
        toks += list(self.dma_last.items())
        for e in self.ENG:
            kn = self.known[e]
            waits = []
            for sk, v in toks:
                if sk == e:
                    continue
                if kn.get(sk, 0) >= v:
                    continue
                kn[sk] = v
                waits.append((sk, v))
            if waits:
                self.ops[e].append((waits, None, None, None))

    def emit(self):
        self.flush()
        nc = self.nc
        fw = list(self.final_toks)
        waited = {e: set() for e in self.ENG}
        for e in self.ENG:
            for waits, fn, inc, pos in self.ops[e]:
                for sk, v in waits:
                    if sk in waited:
                        waited[sk].add(v)
        rank = {e: {p: r + 1 for r, p in enumerate(sorted(waited[e]))} for e in self.ENG}
        with nc.Block() as block:
            def mk(engname, extra=None):
                def f(e):
                    for waits, fn, inc, pos in self.ops[engname]:
                        for sk, v in waits:
                            if sk in rank:
                                v = rank[sk][v]
                            e.wait_ge(self.semh[sk], v)
                        if fn is None:
                            continue
                        ins = getattr(e, fn[0])(**fn[1])
                        if inc is None:
                            continue
                        if pos is None:
                            ins.then_inc(self.semh[inc[0]], inc[1])
                        elif pos in rank[engname]:
                            ins.then_inc(self.semh[engname], 1)
                    if extra:
                        for sk, v in extra:
                            e.wait_ge(self.semh[sk], v)
                return f
            block.tensor(mk("pe"))
            block.scalar(mk("act"))
            block.vector(mk("dve"))
            block.gpsimd(mk("pool"))
            block.sync(mk("sp", fw))
        self.n_milestones = {e: len(waited[e]) for e in self.ENG}


def _rot_cols(base):
    return list(range(base + 8, base + 16)) + list(range(base, base + 8)) + list(range(base + 16, base + 64))


def _win_col_index():
    idx = []
    for c in range(4):
        a = list(range(c * 64, c * 64 + 64)) + list(range((4 + c) * 64, (4 + c) * 64 + 64))
        r = _rot_cols(c * 64) + _rot_cols((4 + c) * 64)
        idx += a + r
    for base in (512, 768, 1024):
        a = list(range(base, base + 128))
        r = _rot_cols(base) + _rot_cols(base + 64)
        idx += a + r
    idx += list(range(640, 768))
    assert len(idx) == 1920
    idx += list(range(896, 1024)) + list(range(1152, 1280)) + list(range(1280, 1304))
    idx += list(range(1304, 1816)) + list(range(1816, 2328))
    idx += list(range(2328, 2840))
    assert len(idx) == IN_W
    return np.asarray(idx)


def _ffn_col_index():
    idx = []
    for j in range(NJ):
        idx += list(range(j * 128, j * 128 + 128)) + list(range(D_FF + j * 128, D_FF + j * 128 + 128))
    return np.asarray(idx)


def _constants():
    bf = ml_dtypes.bfloat16
    c = {}
    c["ident_b"] = np.eye(128, dtype=np.float32).astype(bf)
    c["ident_f"] = np.eye(128, dtype=np.float32)
    inv_freq = (np.float32(500000.0) ** (-np.arange(0, 16, 2, dtype=np.float32) / np.float32(16))).astype(np.float32)
    ang = (np.arange(S_LEN, dtype=np.float32)[:, None] * inv_freq[None, :]).astype(np.float32)
    cos = np.cos(ang).astype(np.float32)
    sin = np.sin(ang).astype(np.float32)
    rc = np.ones((128, S_LEN), np.float32)
    rs = np.zeros((128, S_LEN), np.float32)
    for r in range(128):
        d = r % 64
        if d < 8:
            rc[r] = cos[:, d]
            rs[r] = -sin[:, d]
        elif d < 16:
            rc[r] = cos[:, d - 8]
            rs[r] = sin[:, d - 8]
    c["rope_c"] = rc
    c["rope_s"] = rs
    k = np.arange(128)[:, None]
    q = np.arange(128)[None, :]
    neg = np.float32(-BIG)
    causal = np.where(k > q, neg, 0).astype(np.float32)
    winfar = np.where(k <= q, neg, 0).astype(np.float32)
    c["neg_causal"] = np.tile(causal, (1, 4)).astype(bf)
    c["neg_winfar"] = np.tile(winfar, (1, 4)).astype(bf)
    strict = np.where(k >= q, neg, 0).astype(np.float32)
    nsb = np.zeros((4, 128, 512), np.float32)
    for r in range(4):
        for s in range(4):
            if s < r:
                nsb[r, :, s * 128:(s + 1) * 128] = neg
            elif s == r:
                nsb[r, :, s * 128:(s + 1) * 128] = strict
    c["neg_strict"] = strict.astype(bf)
    e_all = np.zeros((128, S_LEN), np.float32)
    for j in range(32):
        e_all[j, j * 64:(j + 1) * 64] = 1.0
    c["e_all"] = e_all.astype(bf)
    cmp_end = 16 * np.arange(128) + 31
    m = (cmp_end[:, None] <= np.arange(S_LEN)[None, :]).astype(np.float32)
    m[127] = 0
    c["mask_cmp"] = m.astype(bf)
    n_cmp, n_sel = 127, 32
    cs = np.arange(n_cmp) * 16
    ce = cs + 32
    ss = np.arange(n_sel) * 64
    se = ss + 64
    ov = np.clip(np.minimum(ce[:, None], se[None, :]) - np.maximum(cs[:, None], ss[None, :]), 0, None) / 32.0
    va = np.zeros((128, 2, 97), np.float32)
    va[:127, :, 64:96] = ov[:, None, :]
    va[:127, :, 96] = 1.0
    c["vc_aug"] = va.astype(bf)
    tm = np.zeros((128, 16, 32), np.float32)
    ta = np.zeros((128, 16, 32), np.float32)
    for qt in range(16):
        for p in range(128):
            t = qt * 128 + p
            cur = t // 64
            for j in range(32):
                forced = (j == 0) or (j == cur) or (j == cur - 1)
                valid = j * 64 <= t
                if forced:
                    ta[p, qt, j] = BIG
                elif valid:
                    tm[p, qt, j] = 1.0
                else:
                    ta[p, qt, j] = -BIG
    c["topk_m"] = tm
    c["topk_a"] = ta
    u = np.where(np.arange(128)[:, None] >= np.arange(128)[None, :], -1.0, 0.0).astype(np.float32)
    c["u_neg"] = u.astype(bf)
    c["ones_b"] = np.ones((128, 128), np.float32).astype(bf)
    return c


CONST_SPECS = [
    ("ident_b", [128, 128], BF16), ("ident_f", [128, 128], F32),
    ("rope_c", [128, S_LEN], F32), ("rope_s", [128, S_LEN], F32),
    ("neg_causal", [128, 512], BF16), ("neg_winfar", [128, 512], BF16), ("neg_strict", [128, 128], BF16),
    ("e_all", [128, S_LEN], BF16), ("mask_cmp", [128, S_LEN], BF16), ("vc_aug", [128, 2, 97], BF16),
    ("topk_m", [128, 16, 32], F32), ("topk_a", [128, 16, 32], F32),
    ("u_neg", [128, 128], BF16), ("ones_b", [128, 128], BF16),
]

PP_ENG = "pool"
PE_ATT_RATE = 0.52
CP_SLACK = 500.0
NSA_LOCAL = ("e_all", "mask_cmp", "topk_m", "topk_a", "neg_causal", "neg_winfar")

WEIGHT_SPECS = [
    ("w_in_r", [DEPTH, D, IN_W]), ("pos_t", [DEPTH, 2, 64, 32]), ("cmp_w1", [DEPTH, 2, 2048, 128]),
    ("cmp_b1", [DEPTH, 2, 128]), ("cmp_w2", [DEPTH, 2, 128, 64]),
    ("w_mrg", [DEPTH, D, 3072]), ("w_out", [DEPTH, D, D]), ("ln_p", [DEPTH, 4, D]),
    ("w_up_r", [DEPTH, D, 2 * D_FF]), ("conv_p", [DEPTH, 2 * D_FF, 4]), ("w_down", [DEPTH, D_FF, D]),
]


def prep_inputs(inputs):
    g = lambda k: np.asarray(inputs[k], dtype=np.float32)
    w = {}
    w["w_in_r"] = np.ascontiguousarray(g("w_in")[:, :, _win_col_index()])
    w["pos_t"] = np.ascontiguousarray(np.stack([g("cmp_pos_k"), g("cmp_pos_v")], 1).transpose(0, 1, 3, 2))
    w["cmp_w1"] = np.ascontiguousarray(np.stack([g("cmp_w1_k"), g("cmp_w1_v")], 1))
    w["cmp_b1"] = np.ascontiguousarray(np.stack([g("cmp_b1_k"), g("cmp_b1_v")], 1))
    w["cmp_w2"] = np.ascontiguousarray(np.stack([g("cmp_w2_k"), g("cmp_w2_v")], 1))
    win = g("w_in")
    wbr = np.concatenate([g("w_branch_a"), g("w_branch_b")], axis=1)
    blocks = []
    for j in range(8):
        blocks += [win[:, :, 2840 + j * 128:2840 + (j + 1) * 128], win[:, :, 3864 + j * 128:3864 + (j + 1) * 128],
                   wbr[:, :, j * 128:(j + 1) * 128]]
    w["w_mrg"] = np.ascontiguousarray(np.concatenate(blocks, axis=2))
    w["w_out"] = g("w_out")
    w["ln_p"] = np.ascontiguousarray(np.stack([g("ln_mix_g"), g("ln_mix_b"), g("ln_ffn_g"), g("ln_ffn_b")], 1))
    fi = _ffn_col_index()
    w["w_up_r"] = np.ascontiguousarray(g("w_up")[:, :, fi])
    cp = np.concatenate([g("conv_w").transpose(0, 2, 1), g("conv_b")[:, :, None]], axis=2)
    w["conv_p"] = np.ascontiguousarray(cp[:, fi, :])
    w["w_down"] = g("w_down")
    return w


class Prog:
    def __init__(self, layers, debug=(), stop=None, final_from=None):
        self.layers = list(layers)
        self.debug = debug
        self.stop = stop
        _CUR_JOIN[0] = None
        self.nc = bass.Bass("TRN2", target_bir_lowering=False)
        nc = self.nc
        self.x_in = nc.dram_tensor("x_in", [S_LEN, D], F32, kind="ExternalInput").ap()
        self.out = nc.dram_tensor("out", [S_LEN, D], F32, kind="ExternalOutput").ap()
        self.xres = nc.dram_tensor("xres", [S_LEN, D], F32).ap()
        self.cd = {n: nc.dram_tensor("c_" + n, s, dt, kind="ExternalInput").ap() for n, s, dt in CONST_SPECS}
        self.wd = {n: nc.dram_tensor(n, s, F32, kind="ExternalInput").ap() for n, s in WEIGHT_SPECS}
        self.dbg_out = {}
        self.build()

    def sb(self, st, name, shape, dt):
        self._uid = getattr(self, "_uid", 0) + 1
        return st.enter_context(self.nc.sbuf_tensor("%s_%d" % (name, self._uid), shape, dt))

    POOLS = {"a": (0, 1, 2, 3), "b": (4, 5, 6, 7), "z": (0, 1, 2, 3), "c": (4, 5), "acc": (6, 7)}

    def bank(self, pool):
        i = self.bank_i.get(pool, 0)
        self.bank_i[pool] = i + 1
        ids = self.POOLS[pool]
        k = ids[i % len(ids)]
        return self.pbank[k], self.pdep[k]

    def wslot(self):
        i = self.ring_i
        self.ring_i = i + 1
        return self.ring[i % 4], self.ringd[i % 4]

    def dump(self, name, ap, deps, shape, dt=F32):
        if name not in self.debug:
            return
        t = self.nc.dram_tensor("dbg_" + name, shape, dt, kind="ExternalOutput").ap()
        self.dbg_out[name] = "dbg_" + name
        self.S.dma("sp", t, ap, reads=deps, final=True)

    def mm(self, out, lhsT, rhs, start, stop, reads, writes):
        return self.S.op("pe", "matmul", dict(out=out, lhsT=lhsT, rhs=rhs, start=start, stop=stop), reads, writes)

    def tr(self, out, in_, ident, reads, writes):
        return self.S.op("pe", "transpose", dict(out=out, in_=in_, identity=ident), reads, writes)

    def act(self, out, in_, func, reads, writes, scale=1.0, bias=None):
        kw = dict(out=out, in_=in_, func=func, scale=scale)
        if bias is not None:
            kw["bias"] = bias
        return self.S.op("act", "activation", kw, reads, writes)

    def cp(self, eng, out, in_, reads, writes):
        if eng == "act":
            return self.S.op("act", "copy", dict(out=out, in_=in_), reads, writes)
        return self.S.op(eng, "tensor_copy", dict(out=out, in_=in_), reads, writes)

    def tt(self, eng, out, in0, in1, op, reads, writes):
        return self.S.op(eng, "tensor_tensor", dict(out=out, in0=in0, in1=in1, op=op), reads, writes)

    def ts(self, eng, out, in0, s1, s2, op0, op1, reads, writes):
        kw = dict(out=out, in0=in0, scalar1=s1, scalar2=s2, op0=op0)
        if op1 is not None:
            kw["op1"] = op1
        return self.S.op(eng, "tensor_scalar", kw, reads, writes)

    def stt(self, out, in0, scalar, in1, op0, op1, reads, writes):
        return self.S.op("dve", "scalar_tensor_tensor", dict(out=out, in0=in0, scalar=scalar, in1=in1, op0=op0, op1=op1), reads, writes)

    def memset(self, eng, ap, val, writes):
        return self.S.op(eng, "memset", dict(ap=ap, constant=val), (), writes)

    def build(self):
        nc = self.nc
        with ExitStack() as st:
            self.S = S = Sched(nc, st)
            self.xT = self.sb(st, "xT", [128, 8, S_LEN], BF16)
            self.xTd = [Dep() for _ in range(NT)]
            self.ring = [self.sb(st, "ring%d" % i, [128, 4096], BF16) for i in range(4)]
            self.ringd = [Dep() for _ in range(4)]
            self.ring_i = 0
            self.pre = {}
            self.xb_i = 0
            self.ppair = [st.enter_context(nc.psum_tensor("pp%d" % i, [128, 1024], F32)) for i in range(4)]
            self.pbank = [self.ppair[i // 2][:, (i % 2) * 512:(i % 2 + 1) * 512] for i in range(8)]
            self.pdep = [Dep() for _ in range(8)]
            self.bank_i = {}
            C = {}
            cdep = Dep()
            for n, s, dt in CONST_SPECS:
                if n in ("rope_c", "rope_s") or n in NSA_LOCAL:
                    continue
                C[n] = self.sb(st, "k_" + n, s, dt)
                S.dma("sp", C[n][:], self.cd[n], writes=[cdep])
            self.C = C
            self.cdep = cdep
            self.vcaug = C["vc_aug"]
            self.vcaug_d = Dep()
            self.vcaug_d.writer = cdep.writer
            self.lnp = self.sb(st, "lnp", [128, 2, D], F32)
            self.lnp_d = Dep()
            self.lnsm = self.sb(st, "lnsm", [128, 64], F32)
            self.lnsm_d = [Dep(), Dep()]
            self.ln_i = 0
            for k in range(2):
                self.memset("pool", self.lnsm[:, k * 32 + 20:k * 32 + 21], -0.5, [self.lnsm_d[k]])
            self.xres_d = [Dep() for _ in range(NT)]

            first = True
            for li, l in enumerate(self.layers):
                last = (li == len(self.layers) - 1)
                src = self.x_in if first else self.xres
                if first:
                    self.load_xT(src)
                done = False
                with ExitStack() as sa:
                    self.oaT = self.sb(sa, "oaT", [128, 4, S_LEN], BF16)
                    self.obT = self.sb(sa, "obT", [128, 4, S_LEN], BF16)
                    self.oaT_d = [Dep() for _ in range(NT)]
                    self.obT_d = [Dep() for _ in range(NTC)]
                    self.phase_nsa(l)
                    S.soft_barrier()
                    if self.stop is not None and self.stop.startswith("nsa"):
                        done = True
                    if not done:
                        self.phase_sb(l)
                        S.soft_barrier()
                        if self.stop == "sb":
                            done = True
                    if not done:
                        self.phase_merge(l, src)
                        self.dump("xT_m%d" % l, self.xT[:], self.xTd, [128, 8, S_LEN], BF16)
                        S.soft_barrier()
                        if self.stop == "merge":
                            done = True
                if done:
                    break
                self.phase_ffn(l, self.out if last else self.xres)
                if not last:
                    ln_ = self.layers[li + 1]
                    self.prefetch(("nsa", ln_, 4), self.win_view(ln_, OFF_FM_NSA + 4 * 256, 256), (8, 256))
                    self.prefetch(("nsa", ln_, "vc"), self.win_view(ln_, OFF_FM_NSA + 1792, 408), (8, 408))
                S.soft_barrier()
                first = False
            S.emit()

    def load_xT(self, src):
        S = self.S
        with ExitStack() as st:
            xt = [self.sb(st, "ld_x%d" % i, [128, D], F32) for i in range(2)]
            self.xb = [self.sb(st, "xb", [128, D], BF16) for i in range(2)]
            self.xb_d = [Dep(), Dep()]
            xd = [Dep() for _ in range(2)]
            for tt in range(NT):
                b = tt % 2
                S.dma("sp", xt[b][:], src[tt * 128:(tt + 1) * 128, :], writes=[xd[b]])
                self.transpose_to_xT(xt[b], xd[b], tt)
            S.soft_barrier()

    def transpose_to_xT(self, xtile, xdep, tt):
        k = self.xb_i % 2
        self.xb_i += 1
        xb, xbd = self.xb[k], self.xb_d[k]
        self.cp("act", xb[:], xtile[:], [xdep], [xbd])
        pb, pd = self.bank("a")
        pv = pb[:].bitcast(BF16)
        idb = self.C["ident_b"]
        for c in range(8):
            self.tr(pv[:, c * 128:(c + 1) * 128], xb[:, c * 128:(c + 1) * 128], idb[:], [xbd, self.cdep], [pd])
        self.cp("act", self.xT[:, :, tt * 128:(tt + 1) * 128], pv.rearrange("p (c t) -> p c t", c=8), [pd], [self.xTd[tt]])

    def prefetch(self, key, src3, shape):
        self.pre[key] = self.load_w(src3, shape)

    def load_w(self, src3, shape, key=None):
        if key is not None and key in self.pre:
            return self.pre.pop(key)
        slot, dep = self.wslot()
        a, b = shape
        view = slot[:, 0:a * b].rearrange("p (a b) -> p a b", a=a)
        self.S.dma("pool", view, src3, writes=[dep])
        return view, dep

    def proj_fm(self, wv, wdep, col0, tc, pool="a"):
        pb, pd = self.bank(pool)
        rd = [wdep] + self.xTd[tc * 4:(tc + 1) * 4]
        for kc in range(8):
            self.mm(pb[:], wv[:, kc, col0:col0 + 128], self.xT[:, kc, tc * 512:(tc + 1) * 512], kc == 0, kc == 7, rd, [pd])
        return pb, pd

    def proj_tm(self, wv, wdep, col0, ncol, tt):
        pb, pd = self.bank("a")
        rd = [wdep, self.xTd[tt]]
        for kc in range(8):
            self.mm(pb[:, 0:ncol], self.xT[:, kc, tt * 128:(tt + 1) * 128], wv[:, kc, col0:col0 + ncol], kc == 0, kc == 7, rd, [pd])
        return pb, pd

    def win_view(self, l, c0, ncol):
        return self.wd["w_in_r"][l].rearrange("(c p) n -> p c n", p=128)[:, :, c0:c0 + ncol]

    def phase_nsa(self, l):
        S = self.S
        C = dict(self.C)
        with ExitStack() as st:
            for n, s_, dt in CONST_SPECS:
                if n in NSA_LOCAL:
                    C[n] = self.sb(st, "k_" + n, s_, dt)
                    S.dma("sp", C[n][:], self.cd[n], writes=[self.cdep])
            QAz = [self.sb(st, "QAz%d" % g, [128, 4, S_LEN], BF16) for g in range(2)]
            KVT = self.sb(st, "KVT", [128, 4, S_LEN], BF16)
            Vaug = self.sb(st, "Vaug", [128, NT, 4, 65], BF16)
            GA = self.sb(st, "GA", [128, NT, 24], F32)
            qa_d = [Dep() for _ in range(NTC)]
            kv_d = [Dep() for _ in range(4)]
            va_d = [Dep() for _ in range(NT)]
            ga_d = [Dep() for _ in range(NT)]
            for g in range(2):
                self.memset("dve", QAz[g][(1 - g) * 64:(2 - g) * 64, :, :], 0.0, qa_d)
            self.memset("dve", Vaug[:, :, :, 64:65], 1.0, va_d)
            KCMP = self.sb(st, "KCMP", [128, 128], BF16)
            kcmp_d = Dep()
            self.memset("dve", KCMP[:], 0.0, [kcmp_d])
            with ExitStack() as s2:
                ropeC = self.sb(s2, "ropeC", [128, S_LEN], F32)
                ropeS = self.sb(s2, "ropeS", [128, S_LEN], F32)
                rd = Dep()
                S.dma("sp", ropeC[:], self.cd["rope_c"], writes=[rd])
                S.dma("sp", ropeS[:], self.cd["rope_s"], writes=[rd])
                t1 = [self.sb(s2, "rt1", [128, 512], F32) for i in range(2)]
                t2 = [self.sb(s2, "rt2", [128, 512], F32) for i in range(2)]
                t1d = [Dep() for _ in range(2)]
                t2d = [Dep() for _ in range(2)]
                self.rope_it = 0

                def rope_pair(pi):
                    wv, wdep = self.load_w(self.win_view(l, OFF_FM_NSA + pi * 256, 256), (8, 256), key=("nsa", l, pi))
                    for tc in range(NTC):
                        pa, pad = self.proj_fm(wv, wdep, 0, tc)
                        pb_, pbd = self.proj_fm(wv, wdep, 128, tc)
                        b = self.rope_it % 2
                        self.rope_it += 1
                        sl = slice(tc * 512, (tc + 1) * 512)
                        self.tt("dve", t1[b][:], pa[:], ropeC[:, sl], ALU.mult, [pad, rd], [t1d[b]])
                        self.tt("dve", t2[b][:], pb_[:], ropeS[:, sl], ALU.mult, [pbd, rd], [t2d[b]])
                        if pi < 4:
                            for g in range(2):
                                r = slice(g * 64, (g + 1) * 64)
                                self.tt("dve", QAz[g][r, pi, sl], t1[b][r, :], t2[b][r, :], ALU.add, [t1d[b], t2d[b]], [qa_d[tc]])
                        else:
                            self.tt("dve", KVT[:, pi - 4, sl], t1[b][:], t2[b][:], ALU.add, [t1d[b], t2d[b]], [kv_d[pi - 4]])

                rope_pair(4)
                wv, wdep = self.load_w(self.win_view(l, OFF_FM_NSA + 1792, 408), (8, 408), key=("nsa", l, "vc"))
                for tc in range(NTC):
                    pa, pad = self.proj_fm(wv, wdep, 0, tc)
                    self.cp("act", KVT[:, 3, tc * 512:(tc + 1) * 512], pa[:], [pad], [kv_d[3]])
                for tt in range(NT):
                    pa, pad = self.proj_tm(wv, wdep, 128, 280, tt)
                    self.cp("act", Vaug[:, tt, :, 0:64], pa[:, 0:256].rearrange("p (a d) -> p a d", a=4), [pad], [va_d[tt]])
                    self.act(GA[:, tt, :], pa[:, 256:280], AF.Tanh, [pad], [ga_d[tt]], scale=0.5)
                    self.ts("dve", GA[:, tt, :], GA[:, tt, :], 0.5, 0.5, ALU.mult, ALU.add, [ga_d[tt]], [ga_d[tt]])
                posf = self.sb(s2, "posf", [128, 2, 32], F32)
                posb = self.sb(s2, "posb", [128, 2, 32], BF16)
                w2f = self.sb(s2, "w2f", [128, 2, 64], F32)
                w2b = self.sb(s2, "w2b", [128, 2, 64], BF16)
                b1 = self.sb(s2, "b1", [128, 2], F32)
                biasv = self.sb(s2, "biasv", [128, 2], F32)
                hu = [self.sb(s2, "hu", [128, 128], F32) for i in range(2)]
                hsq = [self.sb(s2, "hsq", [128, 128], F32) for i in range(2)]
                hth = [self.sb(s2, "hth", [128, 128], F32) for i in range(2)]
                hidb = [self.sb(s2, "hidb", [128, 128], BF16) for i in range(2)]
                sd = Dep()
                hds = [Dep(), Dep()]
                for kv in range(2):
                    for half in range(2):
                        S.dma("sp", posf[half * 64:(half + 1) * 64, kv, :], self.wd["pos_t"][l, kv], writes=[sd])
                    S.dma("sp", w2f[:, kv, :], self.wd["cmp_w2"][l, kv], writes=[sd])
                    S.dma("sp", b1[:, kv:kv + 1], self.wd["cmp_b1"][l, kv].rearrange("(m o) -> m o", o=1), writes=[sd])
                self.cp("dve", posb[:], posf[:], [sd], [sd])
                self.cp("dve", w2b[:], w2f[:], [sd], [sd])
                N = 127
                hi = 0
                for kv in range(2):
                    slot, wdep = self.wslot()
                    w1s = slot[:, :].rearrange("p (l m) -> p l m", l=32)
                    srcw = self.wd["cmp_w1"][l, kv].rearrange("(l d) m -> d l m", d=64)
                    for half in range(2):
                        S.dma("pool", w1s[half * 64:(half + 1) * 64, :, :], srcw, writes=[wdep])
                    pb, pd = self.bank("b")
                    for li_ in range(32):
                        self.mm(pb[:, 0:2], w1s[0:64, li_, :], posb[0:64, kv, li_:li_ + 1].broadcast_to([64, 2]),
                                li_ == 0, li_ == 31, [wdep, sd], [pd])
                    bvd = Dep()
                    self.tt("dve", biasv[:, kv:kv + 1], pb[:, 0:1], b1[:, kv:kv + 1], ALU.add, [pd, sd], [bvd])
                    for g in range(2):
                        r = slice(g * 64, (g + 1) * 64)
                        ph, phd = self.bank("b")
                        idx = 0 if kv == 0 else 3
                        k_ = hi % 2
                        hi += 1
                        hd = hds[k_]
                        for li_ in range(32):
                            self.mm(ph[:, 0:N], w1s[r, li_, :], KVT[r, idx, li_:li_ + 16 * 126 + 1:16], li_ == 0, li_ == 31, [wdep, kv_d[idx]], [phd])
                        self.ts("dve", hu[k_][:, 0:N], ph[:, 0:N], biasv[:, kv:kv + 1], None, ALU.add, None, [phd, bvd], [hd])
                        self.tt("dve", hsq[k_][:, 0:N], hu[k_][:, 0:N], hu[k_][:, 0:N], ALU.mult, [hd], [hd])
                        self.tt("dve", hsq[k_][:, 0:N], hsq[k_][:, 0:N], hu[k_][:, 0:N], ALU.mult, [hd], [hd])
                        self.stt(hsq[k_][:, 0:N], hsq[k_][:, 0:N], 0.044715, hu[k_][:, 0:N], ALU.mult, ALU.add, [hd], [hd])
                        self.act(hth[k_][:, 0:N], hsq[k_][:, 0:N], AF.Tanh, [hd], [hd], scale=0.7978845608028654)
                        self.stt(hth[k_][:, 0:N], hth[k_][:, 0:N], 1.0, hu[k_][:, 0:N], ALU.add, ALU.mult, [hd], [hd])
                        self.ts("dve", hidb[k_][:, 0:N], hth[k_][:, 0:N], 0.5, None, ALU.mult, None, [hd], [hd])
                        po, pod = self.bank("b")
                        if kv == 0:
                            self.mm(po[r, 0:N], w2b[:, 0, :], hidb[k_][:, 0:N], True, True, [hd, sd], [pod])
                            self.cp("act", KCMP[r, 0:N], po[r, 0:N], [pod], [kcmp_d])
                        else:
                            self.mm(po[0:N, 0:64], hidb[k_][:, 0:N], w2b[:, 1, :], True, True, [hd, sd], [pod])
                            self.cp("act", self.vcaug[0:N, g, 0:64], po[0:N, 0:64], [pod], [self.vcaug_d])
                for pi in (0, 1, 2, 3, 5, 6):
                    rope_pair(pi)
                S.soft_barrier()
            self.dump("qaz0", QAz[0][:], qa_d, [128, 4, S_LEN], BF16)
            self.dump("kvt", KVT[:], kv_d, [128, 4, S_LEN], BF16)
            self.dump("kcmp", KCMP[:], [kcmp_d], [128, 128], BF16)
            self.dump("vcaug", self.vcaug[:], [self.vcaug_d], [128, 2, 97], BF16)
            if self.stop in ("nsa_proj", "nsa_cmp"):
                return
            PT = [self.sb(st, "PT", [128, 512], BF16) for i in range(4)]
            ptd = [Dep() for _ in range(4)]
            self.pt_i = 0
            NP = 3
            oacc = [self.sb(st, "oacc", [128, 8, 64], F32) for i in range(NP)]
            oaccd = [[Dep(), Dep()] for _ in range(NP)]
            tmpc = [self.sb(st, "tmpc", [128, 4, 64], F32) for i in range(2)]
            tmpcd = [Dep() for _ in range(2)]
            self.tci = 0
            sm = [self.sb(st, "sm", [128, 64], F32) for i in range(NP)]
            smd = [[Dep() for _ in range(4)] for _ in range(NP)]
            sc = [self.sb(st, "sc", [128, 32], F32) for i in range(NP)]
            sc2 = [self.sb(st, "sc2", [128, 32], F32) for i in range(NP)]
            negsel = [self.sb(st, "negsel", [128, 32], BF16) for i in range(NP)]
            negselT = [self.sb(st, "negselT", [128, 128], BF16) for i in range(NP)]
            nsd = [Dep() for _ in range(NP)]
            scd = [Dep() for _ in range(NP)]
            for i in range(NP):
                self.memset("dve", negselT[i][:], 0.0, [nsd[i]])

            def evac(accb, accd, W, br, first, qt, g, par):
                ob = qt % NP
                b0 = 16 * br
                sm_ = sm[par]
                sd_ = smd[par][br]
                v = accb[:, 0:4 * W].rearrange("p (h w) -> p h w", h=4)
                self.ts("dve", sm_[:, b0:b0 + 4], v[:, :, W - 1], 1e-30, None, ALU.max, None, [accd], [sd_])
                S.op("dve", "reciprocal", dict(out=sm_[:, b0 + 4:b0 + 8], in_=sm_[:, b0:b0 + 4]), [sd_], [sd_])
                gcol = slice(4 * g * 3 + br, 4 * g * 3 + br + 10, 3)
                self.tt("dve", sm_[:, b0 + 8:b0 + 12], sm_[:, b0 + 4:b0 + 8], GA[:, qt, gcol], ALU.mult, [sd_, ga_d[qt]], [sd_])
                coef = sm_[:, b0 + 8:b0 + 12].unsqueeze(2).broadcast_to([128, 4, 64])
                dst = oacc[ob][:, g * 4:(g + 1) * 4, :]
                od_ = oaccd[ob][g]
                if first:
                    self.tt("dve", dst, v[:, :, 0:64], coef, ALU.mult, [accd, sd_], [od_])
                else:
                    tb = self.tci % 2
                    self.tci += 1
                    self.tt("dve", tmpc[tb][:], v[:, :, 0:64], coef, ALU.mult, [accd, sd_], [tmpcd[tb]])
                    self.tt("dve", dst, dst, tmpc[tb][:], ALU.add, [tmpcd[tb], od_], [od_])
                return v

            def cmp_topk(qt, g, par):
                qs = slice(qt * 128, (qt + 1) * 128)
                qrhs = QAz[g][:, :, qs]
                qdeps = [qa_d[qt // 4]]
                sbk, sbd = self.bank("a")
                self.mm(sbk[0:127, :], KCMP[:, 0:127], qrhs, True, True, qdeps + [kcmp_d], [sbd])
                pi_ = self.pt_i % 4
                self.pt_i += 1
                self.act(PT[pi_][0:127, :], sbk[0:127, :], AF.Exp, [sbd], [ptd[pi_]], scale=0.125)
                ptv = PT[pi_][0:127, :].rearrange("p (h q) -> p h q", h=4)
                self.tt("dve", ptv, ptv, C["mask_cmp"][0:127, qs].unsqueeze(1).broadcast_to([127, 4, 128]), ALU.mult,
                        [ptd[pi_], self.cdep], [ptd[pi_]])
                accb, accd = self.bank("b")
                for h in range(4):
                    self.mm(accb[:, h * 97:(h + 1) * 97], PT[pi_][0:127, h * 128:(h + 1) * 128], self.vcaug[0:127, g, :],
                            h == 0, h == 3, [ptd[pi_], self.vcaug_d], [accd])
                v = evac(accb, accd, 97, 0, True, qt, g, par)
                if qt < 8:
                    return
                sm_, sc_, sd_ = sm[par], sc[par], scd[par]
                s0 = smd[par][0]
                s3 = smd[par][3]
                self.ts("dve", sc_[:], v[:, 0, 64:96], sm_[:, 4:5], None, ALU.mult, None, [accd, s0], [sd_])
                for h in range(1, 4):
                    self.stt(sc_[:], v[:, h, 64:96], sm_[:, 4 + h:5 + h], sc_[:], ALU.mult, ALU.add, [accd, s0, sd_], [sd_])
                self.tt("dve", sc_[:], sc_[:], C["topk_m"][:, qt, :], ALU.mult, [sd_, self.cdep], [sd_])
                self.tt("dve", sc_[:], sc_[:], C["topk_a"][:, qt, :], ALU.add, [sd_, self.cdep], [sd_])
                S.op("dve", "max", dict(out=sm_[:, 48:56], in_=sc_[:]), [sd_], [s3])
                S.op("dve", "match_replace", dict(out=sc2[par][:], in_to_replace=sm_[:, 48:56], in_values=sc_[:], imm_value=-3.0e4),
                     [sd_, s3], [sd_])
                S.op("dve", "max", dict(out=sm_[:, 56:64], in_=sc2[par][:]), [sd_], [s3])
                self.ts("dve", negsel[par][:], sc_[:], sm_[:, 63:64], -BIG, ALU.is_lt, ALU.mult, [sd_, s3], [sd_])
                tb_, tbd = self.bank("a")
                tbv = tb_[:].bitcast(BF16)
                self.tr(tbv[0:32, 0:128], negsel[par][:], C["ident_b"][:], [sd_, self.cdep], [tbd])
                self.cp("act", negselT[par][0:32, :], tbv[0:32, 0:128], [tbd], [nsd[par]])

            def selwin(qt, g, par):
                qs = slice(qt * 128, (qt + 1) * 128)
                qrhs = QAz[g][:, :, qs]
                qdeps = [qa_d[qt // 4]]
                need_sel = qt >= 8
                for br, kidx, vtype in ((2, 2, 1), (1, 1, 0)):
                    kts = list(range(0, qt + 1)) if br == 1 else list(range(max(0, qt - 4), qt + 1))
                    accb, accd = self.bank("b")
                    for ki, kt in enumerate(kts):
                        ks_ = slice(kt * 128, (kt + 1) * 128)
                        extra = []
                        if br == 1 and need_sel:
                            extra.append((C["e_all"][:, ks_], negselT[par][:, :].unsqueeze(1).broadcast_to([128, 4, 128]), [nsd[par], self.cdep]))
                        if kt == qt:
                            extra.append((C["ident_b"][:], C["neg_causal"][:], [self.cdep]))
                        if br == 2 and kt == qt - 4:
                            extra.append((C["ident_b"][:], C["neg_winfar"][:], [self.cdep]))
                        sbk, sbd = self.bank("a")
                        self.mm(sbk[:], KVT[:, kidx, ks_], qrhs, True, len(extra) == 0, qdeps + [kv_d[kidx]], [sbd])
                        for xi, (lh, rh, dd) in enumerate(extra):
                            self.mm(sbk[:], lh, rh, False, xi == len(extra) - 1, dd, [sbd])
                        pi_ = self.pt_i % 4
                        self.pt_i += 1
                        self.act(PT[pi_][:], sbk[:], AF.Exp, [sbd], [ptd[pi_]], scale=0.125)
                        for h in range(4):
                            self.mm(accb[:, h * 65:(h + 1) * 65], PT[pi_][:, h * 128:(h + 1) * 128], Vaug[:, kt, vtype * 2 + g, :],
                                    ki == 0 and h == 0, ki == len(kts) - 1, [ptd[pi_], va_d[kt]], [accd])
                    evac(accb, accd, 65, br, False, qt, g, par)

            S.pe_rate = PE_ATT_RATE
            steps = [(qt, g) for qt in range(NT) for g in range(2)]
            for i in range(2):
                cmp_topk(steps[i][0], steps[i][1], i % NP)
            for i, (qt, g) in enumerate(steps):
                if i + 2 < len(steps):
                    cmp_topk(steps[i + 2][0], steps[i + 2][1], (i + 2) % NP)
                selwin(qt, g, i % NP)
                if g == 1:
                    ob = qt % NP
                    qs = slice(qt * 128, (qt + 1) * 128)
                    tb_, tbd = self.bank("a")
                    for c in range(4):
                        self.tr(tb_[:, c * 128:(c + 1) * 128], oacc[ob][:, 2 * c:2 * c + 2, :].rearrange("p a d -> p (a d)"),
                                C["ident_f"][:], oaccd[ob] + [self.cdep], [tbd])
                    self.cp("act", self.oaT[:, :, qs], tb_[:].rearrange("p (c t) -> p c t", c=4), [tbd], [self.oaT_d[qt]])
            S.pe_rate = 0.42
            self.dump("oaT", self.oaT[:], self.oaT_d, [128, 4, S_LEN], BF16)
            self.prefetch(("sbq", l), self.win_view(l, OFF_FM_SB, 512), (8, 512))
            self.prefetch(("sbk", l), self.win_view(l, OFF_FM_SB + 512, 512), (8, 512))

    def phase_sb(self, l):
        S = self.S
        C = self.C
        with ExitStack() as st:
            QBz = [self.sb(st, "QBz", [128, 4, S_LEN], BF16) for g in range(2)]
            KB = self.sb(st, "KB", [128, 4, S_LEN], BF16)
            VB = self.sb(st, "VB", [128, NT, 512], BF16)
            qb_d = [Dep() for _ in range(NTC)]
            kb_d = Dep()
            vb_d = [Dep() for _ in range(NT)]
            for par in range(2):
                self.memset("dve", QBz[par][(1 - par) * 64:(2 - par) * 64, :, :], 0.0, qb_d)
            wq, wqd = self.load_w(self.win_view(l, OFF_FM_SB, 512), (8, 512), key=("sbq", l))
            wk, wkd = self.load_w(self.win_view(l, OFF_FM_SB + 512, 512), (8, 512), key=("sbk", l))
            for c in range(4):
                for tc in range(NTC):
                    sl = slice(tc * 512, (tc + 1) * 512)
                    pa, pad = self.proj_fm(wq, wqd, c * 128, tc)
                    for par in range(2):
                        r = slice(par * 64, (par + 1) * 64)
                        self.act(QBz[par][r, c, sl], pa[r, :], AF.Copy, [pad], [qb_d[tc]], scale=0.125)
                    pa, pad = self.proj_fm(wk, wkd, c * 128, tc)
                    self.cp("act", KB[:, c, sl], pa[:], [pad], [kb_d])
            wv, wvd = self.load_w(self.win_view(l, OFF_TM_SB, 512), (8, 512))
            for tt in range(NT):
                pa, pad = self.proj_tm(wv, wvd, 0, 512, tt)
                self.cp("act", VB[:, tt, :], pa[:], [pad], [vb_d[tt]])
            self.dump("qbz0", QBz[0][:], qb_d, [128, 4, S_LEN], BF16)
            self.dump("kb", KB[:], [kb_d], [128, 4, S_LEN], BF16)
            E2 = [self.sb(st, "sbE", [128, 2, 512], F32) for i in range(2)]
            Lm2 = [self.sb(st, "sbL", [128, 2, 512], BF16) for i in range(2)]
            tmp2 = [self.sb(st, "sbT", [128, 2, 512], F32) for i in range(2)]
            A2 = [self.sb(st, "sbA", [128, 2, 512], BF16) for i in range(2)]
            carry2 = self.sb(st, "sbC", [128, 2, 512], F32)
            opair = [self.sb(st, "sbO", [128, 4, 128], F32) for i in range(2)]
            Ed = [Dep() for _ in range(2)]
            Ld = [Dep() for _ in range(2)]
            Td = [Dep() for _ in range(2)]
            Ad = [Dep() for _ in range(2)]
            cd_ = Dep()
            od = [Dep() for _ in range(2)]
            oi = 0
            ui = 0
            S.pe_rate = PE_ATT_RATE
            for c in range(4):
                for qc in range(NTC):
                    qsl = slice(qc * 512, (qc + 1) * 512)
                    ob = oi % 2
                    oi += 1
                    ktop = 4 * qc + 3
                    accb, accd = self.bank("acc")
                    first_pv = [True, True]
                    for kt in range(ktop, -1, -1):
                        b = ui % 2
                        ui += 1
                        r = kt - 4 * qc
                        c0 = max(r, 0) * 128
                        cs = slice(c0, 512)
                        ks_ = slice(kt * 128, (kt + 1) * 128)
                        zp = self.ppair[b]
                        zds = [self.pdep[2 * b], self.pdep[2 * b + 1]]
                        zv = zp[:, :].rearrange("p (s n) -> p s n", s=2)[:, :, cs]
                        for par in range(2):
                            zb = self.pbank[2 * b + par]
                            self.mm(zb[:, cs], KB[:, c, ks_], QBz[par][:, c, qc * 512 + c0:(qc + 1) * 512], True, False, [kb_d, qb_d[qc]], [zds[par]])
                            if r >= 0:
                                self.mm(zb[:, c0:c0 + 128], C["ident_b"][:], C["neg_strict"][:], False, False, [self.cdep], [zds[par]])
                        self.act(E2[b][:, :, cs], zv, AF.Exp, zds, [Ed[b]])
                        self.act(Lm2[b][:, :, cs], E2[b][:, :, cs], AF.Ln, [Ed[b]], [Ld[b]], bias=1.0)
                        for par in range(2):
                            zb = self.pbank[2 * b + par]
                            self.mm(zb[:, cs], C["u_neg"][:], Lm2[b][:, par, cs], False, True, [Ld[b], self.cdep], [zds[par]])
                        if kt < ktop:
                            self.tt("dve", tmp2[b][:, :, cs], zv, carry2[:, :, cs], ALU.subtract, zds + [cd_], [Td[b]])
                            self.act(A2[b][:, :, cs], tmp2[b][:, :, cs], AF.Exp, [Td[b]], [Ad[b]])
                        else:
                            self.act(A2[b][:, :, cs], zv, AF.Exp, zds, [Ad[b]])
                        if kt > 0:
                            cp_ = self.ppair[2]
                            cds = [self.pdep[4], self.pdep[5]]
                            cv = cp_[:, :].rearrange("p (s n) -> p s n", s=2)[:, :, cs]
                            for par in range(2):
                                self.mm(self.pbank[4 + par][:, cs], C["ones_b"][:], Lm2[b][:, par, cs], True, True, [Ld[b], self.cdep], [cds[par]])
                            if kt == ktop:
                                self.memset("dve", carry2[:, :, 0:c0], 0.0, [cd_])
                                self.cp("dve", carry2[:, :, cs], cv, cds, [cd_])
                            else:
                                self.tt("dve", carry2[:, :, cs], carry2[:, :, cs], cv, ALU.add, cds + [cd_], [cd_])
                        for par in range(2):
                            h = 2 * c + par
                            self.mm(accb[par * 64:(par + 1) * 64, cs], VB[:, kt, h * 64:(h + 1) * 64], A2[b][:, par, cs],
                                    first_pv[par], kt == 0, [Ad[b], vb_d[kt]], [accd])
                            first_pv[par] = False
                    self.cp("act", self.obT[:, c, qsl], accb[:], [accd], [self.obT_d[qc]])
            S.pe_rate = 0.42
            self.dump("obT", self.obT[:], self.obT_d, [128, 4, S_LEN], BF16)
            wmrg_ = self.wd["w_mrg"][l].rearrange("(c p) n -> p c n", p=128)
            for j in range(2):
                self.prefetch(("mrg", l, 0, j), wmrg_[:, :, j * 384:(j + 1) * 384], (8, 384))

    def ln_tile(self, ybanks, yscale, xr, xrd, tt, dst):
        S = self.S
        k = self.ln_i % 2
        self.ln_i += 1
        sm = self.lnsm[:, k * 32:(k + 1) * 32]
        smd = self.lnsm_d[k]
        r, rdep = xr, xrd
        S.op("act", "mul", dict(out=xr[:], in_=xr[:], mul=ALPHA), [xrd], [xrd])
        for half in range(2):
            yb, yd = ybanks[half]
            hs = slice(half * 512, (half + 1) * 512)
            self.stt(r[:, hs], yb[:], yscale, xr[:, hs], ALU.mult, ALU.add, [yd, xrd], [rdep])
        for half in range(2):
            hs = slice(half * 512, (half + 1) * 512)
            S.op("dve", "bn_stats", dict(out=sm[:, half * 6:(half + 1) * 6], in_=r[:, hs]), [rdep], [smd])
        S.op("dve", "bn_aggr", dict(out=sm[:, 12:14], in_=sm[:, 0:12]), [smd], [smd])
        self.ts("pool", sm[:, 14:15], sm[:, 13:14], EPS, None, ALU.add, None, [smd], [smd])
        self.tt("pool", sm[:, 15:16], sm[:, 14:15], sm[:, 20:21], ALU.pow, [smd], [smd])
        self.ts("dve", sm[:, 16:17], sm[:, 12:13], -1.0, sm[:, 15:16], ALU.mult, ALU.mult, [smd], [smd])
        self.act(r[:], r[:], AF.Identity, [rdep, smd], [rdep], scale=sm[:, 15:16], bias=sm[:, 16:17])
        self.tt("dve", r[:], r[:], self.lnp[:, 0, :], ALU.mult, [rdep, self.lnp_d], [rdep])
        self.tt("dve", r[:], r[:], self.lnp[:, 1, :], ALU.add, [rdep, self.lnp_d], [rdep])
        S.dma("sp", dst[tt * 128:(tt + 1) * 128, :], r[:], reads=[rdep], writes=[self.xres_d[tt]], final=(dst is self.out))
        self.transpose_to_xT(r, rdep, tt)

    def load_lnp(self, l, which):
        for i in range(2):
            self.S.dma("sp", self.lnp[:, i, :], self.wd["ln_p"][l, which * 2 + i].partition_broadcast(128), writes=[self.lnp_d])

    def phase_merge(self, l, src):
        S = self.S
        with ExitStack() as st:
            mT = [self.sb(st, "mT", [128, 8, 512], BF16) for i in range(2)]
            self.xb = [self.sb(st, "xb", [128, D], BF16) for i in range(2)]
            self.xb_d = [Dep(), Dep()]
            mT_d = [Dep() for _ in range(2)]
            wout = self.sb(st, "wout", [128, 8, D], BF16)
            wout_d = Dep()
            wo_src = self.wd["w_out"][l].rearrange("(c p) n -> p c n", p=128)
            for half in range(2):
                S.dma("pool", wout[:, :, half * 512:(half + 1) * 512], wo_src[:, :, half * 512:(half + 1) * 512], writes=[wout_d])
            self.load_lnp(l, 0)
            sg = [self.sb(st, "sg", [128, 512], F32) for i in range(4)]
            sgd = [Dep() for _ in range(4)]
            t12 = [self.sb(st, "t12", [128, 512], F32) for i in range(4)]
            t12d = [Dep() for _ in range(4)]
            xr = [self.sb(st, "xr", [128, D], F32) for i in range(3)]
            xrd = [Dep() for _ in range(3)]
            it = 0
            wmrg = self.wd["w_mrg"][l].rearrange("(c p) n -> p c n", p=128)
            for tc in range(NTC):
                sl = slice(tc * 512, (tc + 1) * 512)
                mb = tc % 2
                for j in range(8):
                    wv, gd = self.load_w(wmrg[:, :, j * 384:(j + 1) * 384], (8, 384), key=("mrg", l, tc, j))
                    gv = wv[:, :, 0:256]
                    bv = wv[:, :, 256:384]
                    k0 = (it % 2) * 2
                    it += 1
                    res = []
                    for i, oT, od_ in ((0, self.oaT, self.oaT_d[tc * 4:(tc + 1) * 4]), (1, self.obT, [self.obT_d[tc]])):
                        pg, pgd = self.proj_fm(gv, gd, i * 128, tc)
                        pA, pAd = self.bank("b")
                        for kc in range(4):
                            self.mm(pA[:], bv[:, i * 4 + kc, :], oT[:, kc, sl], kc == 0, kc == 3, [gd] + od_, [pAd])
                        k = k0 + i
                        self.act(sg[k][:], pg[:], AF.Tanh, [pgd], [sgd[k]], scale=0.5)
                        self.stt(t12[k][:], sg[k][:], 1.0, pA[:], ALU.add, ALU.mult, [sgd[k], pAd], [t12d[k]])
                        res.append(k)
                    self.tt("dve", mT[mb][:, j, :], t12[res[0]][:], t12[res[1]][:], ALU.add, [t12d[res[0]], t12d[res[1]]], [mT_d[mb]])
                for tt in range(tc * 4, tc * 4 + 4):
                    b = tt % 3
                    ts_ = slice(tt * 128, (tt + 1) * 128)
                    tl = slice((tt % 4) * 128, (tt % 4 + 1) * 128)
                    S.dma("sp", xr[b][:], src[ts_, :], reads=[self.xres_d[tt]], writes=[xrd[b]])
                    yb = []
                    for half in range(2):
                        pb, pd = self.bank("b")
                        for j in range(8):
                            self.mm(pb[:], mT[mb][:, j, tl], wout[:, j, half * 512:(half + 1) * 512], j == 0, j == 7, [mT_d[mb], wout_d], [pd])
                        yb.append((pb, pd))
                    self.ln_tile(yb, 0.5, xr[b], xrd[b], tt, self.xres)
            wup_ = self.wd["w_up_r"][l].rearrange("(c p) n -> p c n", p=128)
            for jp in range(2):
                self.prefetch(("up", l, jp), wup_[:, :, jp * 512:(jp + 1) * 512], (8, 512))

    def phase_ffn(self, l, dst):
        S = self.S
        with ExitStack() as st:
            gT = self.sb(st, "gT", [128, NJ, S_LEN], BF16)
            gT_d = [Dep() for _ in range(NTC)]
            convp = self.sb(st, "convp", [128, 2 * NJ, 4], F32)
            cvd = Dep()
            S.dma("sp", convp[:], self.wd["conv_p"][l].rearrange("(j p) k -> p j k", p=128), writes=[cvd])
            wup = self.wd["w_up_r"][l].rearrange("(c p) n -> p c n", p=128)
            with ExitStack() as s2:
                u = [[self.sb(s2, "u", [128, 514], F32) for b in range(2)] for i in range(2)]
                ud = [[Dep() for b in range(2)] for i in range(2)]
                acc = [[self.sb(s2, "acc", [128, 512], F32) for b in range(4)] for i in range(2)]
                accd = [[Dep() for b in range(4)] for i in range(2)]
                th = [self.sb(s2, "th", [128, 512], F32) for b in range(4)]
                thd = [Dep() for b in range(4)]
                pp = [self.sb(s2, "pp", [128, 512], F32) for b in range(4)]
                ppd = [Dep() for b in range(4)]
                ai = 0
                for jp in range(NJ // 2):
                    wv, wdep = self.load_w(wup[:, :, jp * 512:(jp + 1) * 512], (8, 512), key=("up", l, jp))
                    for jj in range(2):
                        j = 2 * jp + jj
                        for tc in range(NTC):
                            b = tc % 2
                            ab = ai % 4
                            ai += 1
                            sl = slice(tc * 512, (tc + 1) * 512)
                            for i in range(2):
                                ch = 2 * j + i
                                pu, pud = self.proj_fm(wv, wdep, jj * 256 + i * 128, tc, pool=("a" if i == 0 else "b"))
                                if tc == 0:
                                    self.memset("pool", u[i][b][:, 0:2], 0.0, [ud[i][b]])
                                else:
                                    self.cp("act", u[i][b][:, 0:2], u[i][1 - b][:, 512:514], [ud[i][1 - b]], [ud[i][b]])
                                self.cp("act", u[i][b][:, 2:514], pu[:], [pud], [ud[i][b]])
                                self.act(acc[i][ab][:], pu[:], AF.Identity, [pud, cvd], [accd[i][ab]], scale=convp[:, ch, 2:3], bias=convp[:, ch, 3:4])
                                self.stt(acc[i][ab][:], u[i][b][:, 1:513], convp[:, ch, 1:2], acc[i][ab][:], ALU.mult, ALU.add,
                                         [ud[i][b], accd[i][ab], cvd], [accd[i][ab]])
                                self.stt(acc[i][ab][:], u[i][b][:, 0:512], convp[:, ch, 0:1], acc[i][ab][:], ALU.mult, ALU.add,
                                         [ud[i][b], accd[i][ab], cvd], [accd[i][ab]])
                            self.act(th[ab][:], acc[0][ab][:], AF.Silu, [accd[0][ab]], [thd[ab]])
                            self.tt("dve", gT[:, j, sl], th[ab][:], acc[1][ab][:], ALU.mult, [thd[ab], accd[1][ab]], [gT_d[tc]])
                wdn = self.wd["w_down"][l].rearrange("(j p) n -> p j n", p=128)
                for si in range(4):
                    v = self.ring[si][:, :].rearrange("p (a b) -> p a b", a=4)
                    S.dma("pool", v, wdn[:, si * 4:(si + 1) * 4, :], writes=[self.ringd[si]])
                S.soft_barrier()
            self.dump("gT", gT[:], gT_d, [128, NJ, S_LEN], BF16)
            wdx = self.sb(st, "wdx", [128, 6, D], BF16)
            self.xb = [self.sb(st, "xb", [128, D], BF16) for i in range(2)]
            self.xb_d = [Dep(), Dep()]
            wdx_d = Dep()
            S.dma("pool", wdx[:], wdn[:, 16:22, :], writes=[wdx_d])
            self.load_lnp(l, 1)
            xr = [self.sb(st, "xr2", [128, D], F32) for i in range(3)]
            xrd = [Dep() for _ in range(3)]
            for tt in range(NT):
                b = tt % 3
                ts_ = slice(tt * 128, (tt + 1) * 128)
                S.dma("sp", xr[b][:], self.xres[ts_, :], reads=[self.xres_d[tt]], writes=[xrd[b]])
                yb = []
                for half in range(2):
                    pb, pd = self.bank("b")
                    hs = slice(half * 512, (half + 1) * 512)
                    for j in range(NJ):
                        if j < 16:
                            wj = self.ring[j // 4][:, (j % 4) * 1024:(j % 4 + 1) * 1024]
                            wjd = self.ringd[j // 4]
                        else:
                            wj = wdx[:, j - 16, :]
                            wjd = wdx_d
                        self.mm(pb[:], gT[:, j, ts_], wj[:, hs], j == 0, j == NJ - 1, [gT_d[tt // 4], wjd], [pd])
                    yb.append((pb, pd))
                self.ln_tile(yb, 1.0, xr[b], xrd[b], tt, dst)


_PROG_CACHE = {}


def _get_prog(layers, debug=(), stop=None):
    key = (tuple(layers), tuple(debug), stop)
    if key not in _PROG_CACHE:
        _PROG_CACHE[key] = Prog(layers, debug, stop)
    return _PROG_CACHE[key]


def kernel(**inputs):
    x = np.asarray(inputs["x"], dtype=np.float32)
    w = prep_inputs(inputs)
    consts = _constants()
    prog = _get_prog(range(DEPTH))
    in_maps = []
    for b in range(8):
        m = {"x_in": np.ascontiguousarray(x[b])}
        for n, _, _ in CONST_SPECS:
            m["c_" + n] = consts[n]
        m.update(w)
        in_maps.append(m)
    res = run_bass_kernel_spmd(prog.nc, in_maps, core_ids=list(range(8)))
    return np.stack([np.asarray(res.results[b]["out"], dtype=np.float32) for b in range(8)], axis=0)
```
